# Optimizing a Trainium2 kernel written in Bass

```python
import math
import jax, jax.numpy as jnp
from jax import lax
import numpy as np

D_MODEL = 1024
BATCH = 8
SEQ = 4096
DEPTH = 2

MIX_WIDTH = D_MODEL
SGU_WIDTH = MIX_WIDTH // 2
SGU_HEADS = 4
SGU_HEAD_DIM = SGU_WIDTH // SGU_HEADS
CHUNK = 128
ATT_WIDTH = MIX_WIDTH - SGU_WIDTH
ATT_HEADS = 8
ATT_HEAD_DIM = ATT_WIDTH // ATT_HEADS
DILATED_PATTERNS = ((128, 1), (512, 4), (2048, 16))
ATT_BLOCK = 128
ROPE_THETA = 10000.0
D_FF = 2816
IN_WIDTH = 2 * SGU_WIDTH + 3 * ATT_WIDTH
N_ADA = 9
EPS = 1e-6

kernel_name = "hybrid_sgu_dilated_macaron_adaln"


def rmsnorm(x, g):
    xf = x.astype(jnp.float32)
    y = xf * lax.rsqrt(jnp.mean(xf * xf, axis=-1, keepdims=True) + EPS)
    return (y * g.astype(jnp.float32)).astype(x.dtype)


def modulate(h, shift, scale):
    return h * (1.0 + scale[:, None, :]) + shift[:, None, :]


def swiglu(y, w_gate, w_up, w_down):
    return (jax.nn.silu(y @ w_gate) * (y @ w_up)) @ w_down


def rope_tables(S, dh, dtype):
    inv = ROPE_THETA ** (-jnp.arange(0, dh, 2, dtype=jnp.float32) / dh)
    ang = jnp.arange(S, dtype=jnp.float32)[:, None] * inv[None, :]
    ang = jnp.concatenate([ang, ang], axis=-1)
    return jnp.cos(ang)[:, None, :].astype(dtype), jnp.sin(ang)[:, None, :].astype(dtype)


def apply_rope(t, cos, sin):
    half = t.shape[-1] // 2
    rot = jnp.concatenate([-t[..., half:], t[..., :half]], axis=-1)
    return t * cos + rot * sin


def spatial_gating(u, v, ln_g, ln_b, w_s, b_s):
    B, S, H, dh = u.shape
    u = jax.nn.gelu(u)
    v = jax.nn.gelu(v)
    vf = v.astype(jnp.float32)
    mu = jnp.mean(vf, axis=-1, keepdims=True)
    var = jnp.mean(jnp.square(vf - mu), axis=-1, keepdims=True)
    vn = ((vf - mu) * lax.rsqrt(var + EPS)).astype(v.dtype) * ln_g + ln_b
    vc = vn.reshape(B, S // CHUNK, CHUNK, H, dh)
    causal = jnp.tril(jnp.ones((CHUNK, CHUNK), dtype=bool))
    ws = jnp.where(causal[None], w_s, jnp.zeros_like(w_s))
    z = jnp.einsum('hij,bnjhc->bnihc', ws, vc) + b_s.T[None, None, :, :, None]
    return u * z.reshape(B, S, H, dh)


def dilated_branch(q, k, v, window, dil):
    B, S, H, dh = q.shape
    n_back = window // dil
    span = dil * ATT_BLOCK
    S_pad = -(-S // span) * span
    L = S_pad // dil
    nb = L // ATT_BLOCK

    def to_blocks(t):
        t = jnp.pad(t, ((0, 0), (0, S_pad - S), (0, 0), (0, 0)))
        t = t.reshape(B, L, dil, H, dh).transpose(0, 2, 3, 1, 4)
        return t.reshape(B, dil, H, nb, ATT_BLOCK, dh)

    def with_prev(t):
        prev = jnp.concatenate([jnp.zeros_like(t[:, :, :, :1]), t[:, :, :, :-1]], axis=3)
        return jnp.concatenate([prev, t], axis=4)

    qb = to_blocks(q)
    kk = with_prev(to_blocks(k))
    vv = with_prev(to_blocks(v))
    s = jnp.einsum('brhnqd,brhnkd->brhnqk', qb, kk,
                   preferred_element_type=jnp.float32) * (1.0 / math.sqrt(dh))
    qi = jnp.arange(ATT_BLOCK)[:, None]
    ki = jnp.arange(2 * ATT_BLOCK)[None, :]
    diff = qi + ATT_BLOCK - ki
    band = (diff >= 0) & (diff <= n_back)
    valid = (jnp.arange(nb)[:, None, None] > 0) | (ki[None] >= ATT_BLOCK)
    mask = band[None] & valid
    s = jnp.where(mask, s, -jnp.inf)
    m = jnp.max(s, axis=-1, keepdims=True)
    p = jnp.exp(s - m)
    den = jnp.sum(p, axis=-1, keepdims=True)
    o = jnp.einsum('brhnqk,brhnkd->brhnqd', p, vv.astype(jnp.float32)) / den
    lse = (m + jnp.log(den))[..., 0]
    o = o.reshape(B, dil, H, L, dh).transpose(0, 3, 1, 2, 4).reshape(B, S_pad, H, dh)[:, :S]
    lse = lse.reshape(B, dil, H, L).transpose(0, 3, 1, 2).reshape(B, S_pad, H)[:, :S]
    return o, lse


def dilated_mixture(q, k, v):
    outs, lses = [], []
    for window, dil in DILATED_PATTERNS:
        o, lse = dilated_branch(q, k, v, window, dil)
        outs.append(o)
        lses.append(lse)
    w = jax.nn.softmax(jnp.stack(lses, axis=0), axis=0)
    o = jnp.sum(w[..., None] * jnp.stack(outs, axis=0), axis=0)
    return o.astype(q.dtype)


def setup_inputs(seed: int = 0) -> dict:
    key = jax.random.key(seed)
    ks = jax.random.split(key, 20)
    f32 = jnp.float32
    nrm = lambda k, shape, scale: jax.random.normal(k, shape, f32) * scale
    return {
        "x": nrm(ks[0], (BATCH, SEQ, D_MODEL), 1.0),
        "c": nrm(ks[1], (BATCH, D_MODEL), 1.0),
        "ada_w": nrm(ks[2], (DEPTH, D_MODEL, N_ADA * D_MODEL), D_MODEL ** -0.5),
        "ada_b": nrm(ks[3], (DEPTH, N_ADA * D_MODEL), 0.02),
        "norm_g": 1.0 + nrm(ks[4], (DEPTH, 3, D_MODEL), 0.02),
        "ffn1_wg": nrm(ks[5], (DEPTH, D_MODEL, D_FF), D_MODEL ** -0.5),
        "ffn1_wu": nrm(ks[6], (DEPTH, D_MODEL, D_FF), D_MODEL ** -0.5),
        "ffn1_wd": nrm(ks[7], (DEPTH, D_FF, D_MODEL), D_FF ** -0.5),
        "ffn2_wg": nrm(ks[8], (DEPTH, D_MODEL, D_FF), D_MODEL ** -0.5),
        "ffn2_wu": nrm(ks[9], (DEPTH, D_MODEL, D_FF), D_MODEL ** -0.5),
        "ffn2_wd": nrm(ks[10], (DEPTH, D_FF, D_MODEL), D_FF ** -0.5),
        "w_in": nrm(ks[11], (DEPTH, D_MODEL, IN_WIDTH), D_MODEL ** -0.5),
        "sgu_ln_g": 1.0 + nrm(ks[12], (DEPTH, SGU_HEADS, SGU_HEAD_DIM), 0.02),
        "sgu_ln_b": nrm(ks[13], (DEPTH, SGU_HEADS, SGU_HEAD_DIM), 0.02),
        "sgu_w": nrm(ks[14], (DEPTH, SGU_HEADS, CHUNK, CHUNK), CHUNK ** -0.5),
        "sgu_b": 1.0 + nrm(ks[15], (DEPTH, SGU_HEADS, CHUNK), 0.02),
        "w_out": nrm(ks[16], (DEPTH, MIX_WIDTH, D_MODEL), MIX_WIDTH ** -0.5),
        "final_g": 1.0 + nrm(ks[17], (D_MODEL,), 0.02),
    }


def reference(x, c, ada_w, ada_b, norm_g, ffn1_wg, ffn1_wu, ffn1_wd, ffn2_wg, ffn2_wu, ffn2_wd,
              w_in, sgu_ln_g, sgu_ln_b, sgu_w, sgu_b, w_out, final_g):
    B, S, D = x.shape
    cos, sin = rope_tables(S, ATT_HEAD_DIM, x.dtype)
    c_act = jax.nn.silu(c)
    h = x
    for l in range(DEPTH):
        mod = c_act @ ada_w[l] + ada_b[l]
        sh1, sc1, g1, sh2, sc2, g2, sh3, sc3, g3 = jnp.split(mod, N_ADA, axis=-1)

        y = modulate(rmsnorm(h, norm_g[l, 0]), sh1, sc1)
        h = h + 0.5 * g1[:, None, :] * swiglu(y, ffn1_wg[l], ffn1_wu[l], ffn1_wd[l])

        y = modulate(rmsnorm(h, norm_g[l, 1]), sh2, sc2)
        proj = y @ w_in[l]
        u_a, v_a, q_b, k_b, v_b = jnp.split(
            proj, [SGU_WIDTH, 2 * SGU_WIDTH, 2 * SGU_WIDTH + ATT_WIDTH, 2 * SGU_WIDTH + 2 * ATT_WIDTH], axis=-1)
        u_a = u_a.reshape(B, S, SGU_HEADS, SGU_HEAD_DIM)
        v_a = v_a.reshape(B, S, SGU_HEADS, SGU_HEAD_DIM)
        out_a = spatial_gating(u_a, v_a, sgu_ln_g[l], sgu_ln_b[l], sgu_w[l], sgu_b[l])
        q_b = apply_rope(q_b.reshape(B, S, ATT_HEADS, ATT_HEAD_DIM), cos, sin)
        k_b = apply_rope(k_b.reshape(B, S, ATT_HEADS, ATT_HEAD_DIM), cos, sin)
        v_b = v_b.reshape(B, S, ATT_HEADS, ATT_HEAD_DIM)
        out_b = dilated_mixture(q_b, k_b, v_b)
        mixed = jnp.concatenate([out_a.reshape(B, S, SGU_WIDTH), out_b.reshape(B, S, ATT_WIDTH)], axis=-1)
        h = h + g2[:, None, :] * (mixed @ w_out[l])

        y = modulate(rmsnorm(h, norm_g[l, 2]), sh3, sc3)
        h = h + 0.5 * g3[:, None, :] * swiglu(y, ffn2_wg[l], ffn2_wu[l], ffn2_wd[l])
    return rmsnorm(h, final_g)
```

```python
import numpy as np
from contextlib import ExitStack
import concourse.bass as bass
import concourse.mybir as mybir
from concourse.bass_utils import run_bass_kernel_spmd

F32 = mybir.dt.float32
BF16 = mybir.dt.bfloat16
AF = mybir.ActivationFunctionType
ALU = mybir.AluOpType
AX = mybir.AxisListType

D = 1024
S = 4096
DEPTH = 2
DFF = 2816
KC = 8
FC = 22
T = 512
NT = S // T
NTR = NT
INW = 2560
NADA = 9
EPS = 1e-6
NEG = -30000.0


class Buf:
    __slots__ = ("w", "r", "name")

    def __init__(self, name="", init=None):
        self.w = {}
        self.r = dict(init) if init else {}
        self.name = name


def _merge(d, tok):
    k = (tok[0], tok[1])
    if d.get(k, 0) < tok[2]:
        d[k] = tok[2]


class Sched:
    ENG = ["pe", "act", "dve", "pool", "sp"]

    def __init__(self, nc, stack):
        self.nc = nc
        self.stack = stack
        self.q = {e: [] for e in self.ENG}
        self.sem = {e: stack.enter_context(nc.semaphore("prog_" + e)) for e in self.ENG}
        self.cnt = {e: 0 for e in self.ENG}
        self.seen = {e: {} for e in self.ENG}
        self.dsem = {}
        self.dcnt = {}
        self.nwait = 0
        self.sym = {e: [] for e in self.ENG}

    def _waits(self, eng, deps):
        seen = self.seen[eng]
        for (kind, key), n in deps.items():
            if kind == "e" and key == eng and eng == "pe":
                continue
            if seen.get((kind, key), 0) >= n:
                continue
            seen[(kind, key)] = n
            s = self.sem[key] if kind == "e" else self.dsem[key]
            self.q[eng].append(lambda e, s=s, n=n: e.wait_ge(s, n))
            self.sym[eng].append(("w", (kind, key), n))
            self.nwait += 1

    def _deps(self, reads, writes, accw, extra):
        deps = {}
        for t in extra:
            if t is not None:
                _merge(deps, t)
        for b in reads:
            for k, n in b.w.items():
                _merge(deps, (k[0], k[1], n))
        for b in writes:
            for k, n in b.w.items():
                _merge(deps, (k[0], k[1], n))
            for k, n in b.r.items():
                _merge(deps, (k[0], k[1], n))
        for b in accw:
            for k, n in b.r.items():
                _merge(deps, (k[0], k[1], n))
        return deps

    def _mark(self, tok, reads, writes, accw):
        for b in reads:
            _merge(b.r, tok)
        for b in writes:
            b.w = {(tok[0], tok[1]): tok[2]}
            b.r = {}
        for b in accw:
            _merge(b.w, tok)

    def op(self, eng, fn, reads=(), writes=(), accw=(), sig=True, extra=()):
        deps = self._deps(reads, writes, accw, extra)
        self._waits(eng, deps)
        if sig:
            self.cnt[eng] += 1
            s = self.sem[eng]
            self.q[eng].append(lambda e, fn=fn, s=s: fn(e).then_inc(s, 1))
            self.sym[eng].append(("i", ("e", eng), 1))
            tok = ("e", eng, self.cnt[eng])
        else:
            self.q[eng].append(fn)
            tok = ("e", eng, self.cnt[eng] + 1)
        self._mark(tok, reads, writes, accw)
        return tok

    def dma(self, eng, key, fn, reads=(), writes=(), accw=(), extra=()):
        if key not in self.dsem:
            self.dsem[key] = self.stack.enter_context(self.nc.semaphore("d_" + key))
            self.dcnt[key] = 0
        deps = self._deps(reads, writes, accw, extra)
        self._waits(eng, deps)
        self.dcnt[key] += 16
        s = self.dsem[key]
        self.q[eng].append(lambda e, fn=fn, s=s: fn(e).then_inc(s, 16))
        self.sym[eng].append(("i", ("d", key), 16))
        tok = ("d", key, self.dcnt[key])
        self._mark(tok, reads, writes, accw)
        return tok

    def check_deadlock(self):
        val = {}
        pc = {e: 0 for e in self.ENG}
        progress = True
        while progress:
            progress = False
            for e in self.ENG:
                q = self.sym[e]
                while pc[e] < len(q):
                    kind, key, n = q[pc[e]]
                    if kind == "w":
                        if val.get(key, 0) >= n:
                            pc[e] += 1
                            progress = True
                        else:
                            break
                    else:
                        val[key] = val.get(key, 0) + n
                        pc[e] += 1
                        progress = True
        stuck = {e: (pc[e], len(self.sym[e]), self.sym[e][pc[e]] if pc[e] < len(self.sym[e]) else None) for e in self.ENG}
        ok = all(pc[e] == len(self.sym[e]) for e in self.ENG)
        return ok, stuck, val

    def fence(self):
        f = {}
        for e in self.ENG:
            if self.cnt[e]:
                f[("e", e)] = self.cnt[e]
        for k, n in self.dcnt.items():
            if n:
                f[("d", k)] = n
        return f

    def wait_all(self, eng, toks):
        deps = {}
        for t in toks:
            _merge(deps, t)
        self._waits(eng, deps)

    def run(self, block):
        q = self.q

        @block.tensor
        def _(e):
            for f in q["pe"]:
                f(e)

        @block.scalar
        def _(e):
            for f in q["act"]:
                f(e)

        @block.vector
        def _(e):
            for f in q["dve"]:
                f(e)

        @block.gpsimd
        def _(e):
            for f in q["pool"]:
                f(e)

        @block.sync
        def _(e):
            for f in q["sp"]:
                f(e)


def build_nc(stop_after=None, debug=False):
    nc = bass.Bass("TRN2", target_bir_lowering=False)

    def din(name, shape, dt=F32):
        return nc.dram_tensor(name, list(shape), dt, kind="ExternalInput").ap()

    xT = din("xT", [D, S])
    cv = din("cv", [128, KC])
    ada_w = din("ada_w", [DEPTH, D, NADA * D])
    ada_b = din("ada_b", [128, DEPTH * 72])
    ngv = din("ngv", [128, DEPTH * 3 * KC])
    fgv = din("fgv", [128, KC])
    wg_d = [din("ffn1_wg", [DEPTH, D, DFF]), din("ffn2_wg", [DEPTH, D, DFF])]
    wu_d = [din("ffn1_wu", [DEPTH, D, DFF]), din("ffn2_wu", [DEPTH, D, DFF])]
    wd_d = [din("ffn1_wd", [DEPTH, DFF, D]), din("ffn2_wd", [DEPTH, DFF, D])]
    win_d = din("w_in", [DEPTH, D, INW])
    wsT_d = din("wsT", [DEPTH, 128, 4, 128])
    lng_d = din("lng_b", [DEPTH, 128, 512])
    lnb_d = din("lnb_b", [DEPTH, 128, 512])
    bs_d = din("bs_row", [DEPTH, 1, 512])
    wout_d = din("w_out", [DEPTH, D, D])
    rope_d = din("rope", [2, 128, S])
    tril_d = din("tril", [128, 128])
    mb_d = din("mbias", [4, 128, 512])
    ident_d = din("ident", [128, 128])
    sel_d = din("sel65", [128, 64])
    perm_d = din("permh", [128, 128])

    outT = nc.dram_tensor("outT", [D, S], F32, kind="ExternalOutput").ap()
    dk = "ExternalOutput" if debug else None

    def dscr(name, shape, dt):
        if debug:
            return nc.dram_tensor(name, list(shape), dt, kind="ExternalOutput").ap()
        return nc.dram_tensor(name, list(shape), dt).ap()

    hS = dscr("hS", [D, S], F32)
    qS = dscr("qS", [512, S], BF16)
    kS = dscr("kS", [512, S], BF16)
    vS = dscr("vS", [S, 520], BF16)
    mS = dscr("mS", [D, S], BF16)
    if debug:
        dbg_mod = nc.dram_tensor("dbg_mod", [128, DEPTH * 72], F32, kind="ExternalOutput").ap()

    AW = 53000
    with ExitStack() as st:
        Sx = Sched(nc, st)
        arena = st.enter_context(nc.sbuf_tensor("arena", [128, AW], F32))
        top = [0]
        hiw = [0]
        cur_fence = [None]

        def sb(name, shape, dt, stack=None):
            nel = 1
            for d_ in shape[1:]:
                nel *= d_
            nw = (nel * (4 if dt == F32 else 2) + 3) // 4
            nw = (nw + 7) // 8 * 8
            off = top[0]
            top[0] += nw
            hiw[0] = max(hiw[0], top[0])
            assert top[0] <= AW, ("SBUF arena overflow", name, top[0])
            a = arena[0:shape[0], off:off + nw]
            if dt != F32:
                a = a.bitcast(dt)
            a = a[:, 0:nel]
            if len(shape) > 2:
                names = " ".join("d%d" % i_ for i_ in range(1, len(shape)))
                kw = {"d%d" % i_: shape[i_] for i_ in range(1, len(shape) - 1)}
                a = a.rearrange("p (%s) -> p %s" % (names, names), **kw)
            return a

        class arena_scope:
            def __enter__(self):
                self.mark = top[0]
                return self

            def __exit__(self, *a):
                top[0] = self.mark
                cur_fence[0] = Sx.fence()
                return False

        def NB(name=""):
            return Buf(name, init=cur_fence[0])

        def cust(ap, rel, free, nparts=128):
            return bass.AP(ap.tensor, ap.offset + rel, [[ap.ap[0][0], nparts]] + [list(x) for x in free])

        modv = sb("modv", [128, DEPTH * 72], F32)
        Av = sb("Av", [128, DEPTH * 3 * KC], F32)
        Gv = sb("Gv", [128, DEPTH * 3 * KC], F32)
        ng_sb = sb("ng_sb", [128, DEPTH * 3 * KC], F32)
        fg_sb = sb("fg_sb", [128, KC], F32)
        adab_sb = sb("adab_sb", [128, DEPTH * 72], F32)
        c_sb = sb("c_sb", [128, KC], F32)
        cact = sb("cact", [128, KC], BF16)
        ones_bf = sb("ones_bf", [128, 128], BF16)
        ident_bf = sb("ident_bf", [128, 128], BF16)
        sel_sb = sb("sel_sb", [128, 64], F32)
        perm_bf = sb("perm_bf", [128, 128], BF16)
        ones_row = sb("ones_row", [1, 128], F32)
        eps_sb = sb("eps_sb", [128, 1], F32)
        tril_sb = sb("tril_sb", [128, 128], F32)
        ps = st.enter_context(nc.psum_tensor("ps", [128, 8, 512], F32))
        psB = [Buf("ps%d" % i) for i in range(8)]
        block = st.enter_context(nc.Block())

        bConst = Buf("const")
        bMod = Buf("mod")
        bH = [Buf("H%d" % t) for t in range(NT)]
        bQd, bKd, bVd = Buf("qS"), Buf("kS"), Buf("vS")
        bMd = [Buf("mS%d" % t) for t in range(NT)]
        out_toks = []

        with arena_scope():
            ident_f = sb("ident_f", [128, 128], F32)
            perm_f = sb("perm_f", [128, 128], F32)
            for dst, src in ((ng_sb, ngv), (fg_sb, fgv), (adab_sb, ada_b), (c_sb, cv), (ident_f, ident_d),
                             (sel_sb, sel_d), (tril_sb, tril_d)):
                Sx.dma("sp", "const", lambda e, d=dst, s=src: e.dma_start(out=d[:], in_=s), accw=[bConst])
            Sx.op("pool", lambda e: e.memset(ones_bf[:], 1.0), accw=[bConst])
            Sx.op("pool", lambda e: e.memset(ones_row[:], 1.0), accw=[bConst])
            Sx.op("pool", lambda e: e.memset(eps_sb[:], EPS), accw=[bConst])
            Sx.dma("sp", "const", lambda e: e.dma_start(out=perm_f[:], in_=perm_d), accw=[bConst])
            Sx.op("dve", lambda e: e.tensor_copy(out=ident_bf[:], in_=ident_f[:]), reads=[bConst], accw=[bConst])
            Sx.op("dve", lambda e: e.tensor_copy(out=perm_bf[:], in_=perm_f[:]), reads=[bConst], accw=[bConst])

        def bank(i):
            return ps[:, i, :]

        def norm_chain_a(hbuf, bHb, sq, bSq, pst_i):
            Sx.op("pool", lambda e: e.tensor_tensor(out=sq[:].rearrange("p k t -> p (k t)"),
                                                    in0=hbuf[:].rearrange("p k t -> p (k t)"),
                                                    in1=hbuf[:].rearrange("p k t -> p (k t)"), op=ALU.mult),
                  reads=[bHb], writes=[bSq])
            for kc in range(KC):
                Sx.op("pe", lambda e, kc=kc: e.matmul(bank(pst_i), lhsT=ones_bf[:], rhs=sq[:, kc, :],
                                                      start=(kc == 0), stop=(kc == KC - 1)),
                      reads=[bSq, bConst], writes=[psB[pst_i]] if kc == 0 else (), accw=[psB[pst_i]] if kc else (),
                      sig=(kc == KC - 1))

        def norm_chain_rstd(pst_i, lnt, bLn, rstd, bRs):
            Sx.op("act", lambda e: e.activation(out=lnt[:], in_=bank(pst_i), func=AF.Ln, bias=eps_sb[:],
                                                scale=1.0 / D),
                  reads=[psB[pst_i], bConst], writes=[bLn])
            Sx.op("act", lambda e: e.activation(out=rstd[:], in_=lnt[:], func=AF.Exp, scale=-0.5),
                  reads=[bLn], writes=[bRs])

        def modulate(hbuf, bHb, rstd, bRs, tmps, bTmps, yT, bY, l, i):
            base = (l * 3 + i) * KC
            shbase = l * 72 + (3 * i) * KC
            for kc in range(KC):
                s = kc % len(tmps)
                Sx.op("dve", lambda e, kc=kc, s=s: e.scalar_tensor_tensor(
                    out=tmps[s][:], in0=hbuf[:, kc, :], scalar=Av[:, base + kc:base + kc + 1], in1=rstd[:],
                    op0=ALU.mult, op1=ALU.mult), reads=[bHb, bRs, bMod], writes=[bTmps[s]])
                Sx.op("act", lambda e, kc=kc, s=s: e.activation(
                    out=yT[:, kc, :], in_=tmps[s][:], func=AF.Identity,
                    bias=modv[:, shbase + kc:shbase + kc + 1], scale=1.0), reads=[bTmps[s], bMod], writes=[bY[kc]])

        def load_weight(dst, src2d, nchunk, rows, key, bW):
            first = True
            for k in range(nchunk):
                Sx.dma("pool", key, lambda e, k=k: e.dma_start(out=dst[:, k, :], in_=src2d[k * 128:(k + 1) * 128, :]),
                       writes=[bW] if first else (), accw=() if first else [bW])
                first = False

        with arena_scope() as ph:
            wa = [sb("wa%d" % i, [128, KC, 512], BF16, ph) for i in range(3)]
            bWa = [NB() for _ in range(3)]
            Sx.op("act", lambda e: e.activation(out=cact[:], in_=c_sb[:], func=AF.Silu), reads=[bConst], accw=[bConst])
            gi = 0
            for l in range(DEPTH):
                src = ada_w[l].rearrange("(k p) f -> p k f", p=128)
                for g in range(18):
                    s = gi % 3
                    gi += 1
                    Sx.dma("pool", "wa%d" % s, lambda e, s=s, g=g, src=src: e.dma_start(
                        out=wa[s][:], in_=src[:, :, g * 512:(g + 1) * 512]), writes=[bWa[s]])
                    for jj in range(4):
                        col = l * 72 + g * 4 + jj
                        for kc in range(KC):
                            Sx.op("pe", lambda e, s=s, jj=jj, kc=kc, col=col: e.matmul(
                                ps[:, 0, col:col + 1], lhsT=wa[s][:, kc, jj * 128:(jj + 1) * 128],
                                rhs=cact[:, kc:kc + 1], start=(kc == 0), stop=(kc == KC - 1)),
                                reads=[bWa[s], bConst], accw=[psB[0]], sig=(kc == KC - 1))
            Sx.op("dve", lambda e: e.tensor_tensor(out=modv[:], in0=ps[:, 0, 0:DEPTH * 72], in1=adab_sb[:], op=ALU.add),
                  reads=[psB[0], bConst], writes=[bMod])
            for l in range(DEPTH):
                for i in range(3):
                    b0 = (l * 3 + i) * KC
                    sc0 = l * 72 + (3 * i + 1) * KC
                    g0 = l * 72 + (3 * i + 2) * KC
                    Sx.op("dve", lambda e, b0=b0, sc0=sc0: e.scalar_tensor_tensor(
                        out=Av[:, b0:b0 + KC], in0=modv[:, sc0:sc0 + KC], scalar=1.0, in1=ng_sb[:, b0:b0 + KC],
                        op0=ALU.add, op1=ALU.mult), reads=[bMod, bConst], accw=[bMod])
                    Sx.op("dve", lambda e, b0=b0, g0=g0, i=i: e.tensor_scalar(
                        out=Gv[:, b0:b0 + KC], in0=modv[:, g0:g0 + KC], scalar1=(1.0 if i == 1 else 0.5), scalar2=None,
                        op0=ALU.mult), reads=[bMod], accw=[bMod])
            if debug:
                out_toks.append(Sx.dma("sp", "dbg", lambda e: e.dma_start(out=dbg_mod, in_=modv[:]), reads=[bMod]))

        def hsrc_tile(src, t):
            return src.rearrange("(k p) s -> p k s", p=128)[:, :, t * T:(t + 1) * T]

        def residual_phase_tail(m, t, src, dst, hres, bHres, ps_i, gcol, is_last_reader_buf=None):
            pass

        def ffn_phase(l, which, src, pre=None):
            i = 0 if which == 0 else 2
            with arena_scope() as ph:
                Wg = sb("Wg", [128, KC, DFF], BF16, ph)
                Wu = sb("Wu", [128, KC, DFF], BF16, ph)
                Wd = sb("Wd", [128, FC, D], BF16, ph)
                bWg, bWu, bWd = NB(), NB(), NB()
                if pre is not None:
                    pre()
                CBS = [(0, 6), (6, 12), (12, 17), (17, 22)]
                bWgc = [NB() for _ in CBS]
                bWuc = [NB() for _ in CBS]
                cb_of = {}
                for ci_, (j0, j1) in enumerate(CBS):
                    for j_ in range(j0, j1):
                        cb_of[j_] = ci_
                    for (Wx, wsrc, bWx, kname) in ((Wg, wg_d[which][l], bWgc, "wg%d"), (Wu, wu_d[which][l], bWuc, "wu%d")):
                        for k in range(KC):
                            Sx.dma("pool", kname % ci_, lambda e, k=k, j0=j0, j1=j1, Wx=Wx, wsrc=wsrc: e.dma_start(
                                out=Wx[:, k, j0 * 128:j1 * 128], in_=wsrc[k * 128:(k + 1) * 128, j0 * 128:j1 * 128]),
                                writes=[bWx[ci_]] if k == 0 else (), accw=[bWx[ci_]] if k else ())
                load_weight(Wd, wd_d[which][l], FC, DFF, "wd", bWd)
                hbuf = sb("hbuf", [128, KC, T], F32, ph)
                sq = sb("sq", [128, KC, T], BF16, ph)
                yT = sb("yT", [128, KC, T], BF16, ph)
                actT = sb("actT", [128, FC, T], BF16, ph)
                tmps = [sb("tmp%d" % j, [128, T], F32, ph) for j in range(2)]
                sgs = [sb("sg%d" % j, [128, T], F32, ph) for j in range(2)]
                rstd = sb("rstd", [128, T], F32, ph)
                hres = [sb("hres%d" % j, [128, T], F32, ph) for j in range(2)]
                bHb, bSq, bLn, bRs = NB(), NB(), NB(), NB()
                bY = [NB() for _ in range(KC)]
                bAct = [NB() for _ in range(FC)]
                bTmps = [NB() for _ in range(2)]
                bSg = [NB() for _ in range(2)]
                bHres = [NB() for _ in range(2)]
                PST, PG, PU, PD = 0, (1, 2), (3, 4), (5, 6)

                def load_h(t):
                    Sx.dma("sp", "h", lambda e, t=t: e.dma_start(out=hbuf[:], in_=hsrc_tile(src, t)),
                           reads=[bH[t]] if src is hS else (), writes=[bHb])

                def chain(t):
                    norm_chain_a(hbuf, bHb, sq, bSq, PST)

                def chain_b(t):
                    norm_chain_rstd(PST, rstd, bRs, rstd, bRs)
                    modulate(hbuf, bHb, rstd, bRs, tmps, bTmps, yT, bY, l, i)

                load_h(0)
                chain(0)
                chain_b(0)
                srcr = src.rearrange("(k p) s -> k p s", p=128)
                dstr = hS.rearrange("(k p) s -> k p s", p=128)
                gbase = (l * 3 + i) * KC
                hrc = 0
                for t in range(NTR):
                    if t + 1 < NTR:
                        load_h(t + 1)
                    for j in range(FC):
                        pg, pu = PG[j % 2], PU[j % 2]
                        for kc in range(KC):
                            Sx.op("pe", lambda e, j=j, kc=kc, pg=pg: e.matmul(
                                bank(pg), lhsT=Wg[:, kc, j * 128:(j + 1) * 128], rhs=yT[:, kc, :],
                                start=(kc == 0), stop=(kc == KC - 1)),
                                reads=[bWgc[cb_of[j]], bY[kc]], writes=[psB[pg]] if kc == 0 else (),
                                accw=[psB[pg]] if kc else (), sig=(kc == KC - 1))
                        for kc in range(KC):
                            Sx.op("pe", lambda e, j=j, kc=kc, pu=pu: e.matmul(
                                bank(pu), lhsT=Wu[:, kc, j * 128:(j + 1) * 128], rhs=yT[:, kc, :],
                                start=(kc == 0), stop=(kc == KC - 1)),
                                reads=[bWuc[cb_of[j]], bY[kc]], writes=[psB[pu]] if kc == 0 else (),
                                accw=[psB[pu]] if kc else (), sig=(kc == KC - 1))
                        sgi = j % 2
                        Sx.op("act", lambda e, pg=pg, sgi=sgi: e.activation(out=sgs[sgi][:], in_=bank(pg), func=AF.Silu),
                              reads=[psB[pg]], writes=[bSg[sgi]])
                        Sx.op("dve", lambda e, j=j, pu=pu, sgi=sgi: e.tensor_tensor(
                            out=actT[:, j, :], in0=bank(pu), in1=sgs[sgi][:], op=ALU.mult),
                            reads=[psB[pu], bSg[sgi]], writes=[bAct[j]])
                    if t + 1 < NTR:
                        chain(t + 1)
                    def load_hres(m, t=t):
                        hs = m % 2
                        Sx.dma("sp", "hres%d" % hs, lambda e, m=m, t=t, hs=hs: e.dma_start(
                            out=hres[hs][:], in_=srcr[m, :, t * T:(t + 1) * T]),
                            reads=[bH[t]] if src is hS else (), writes=[bHres[hs]])

                    load_hres(0)
                    for m in range(KC):
                        pd = PD[m % 2]
                        hs = m % 2
                        if m + 1 < KC:
                            load_hres(m + 1)
                        for j in range(FC):
                            Sx.op("pe", lambda e, j=j, m=m, pd=pd: e.matmul(
                                bank(pd), lhsT=Wd[:, j, m * 128:(m + 1) * 128], rhs=actT[:, j, :],
                                start=(j == 0), stop=(j == FC - 1)),
                                reads=[bWd, bAct[j]], writes=[psB[pd]] if j == 0 else (),
                                accw=[psB[pd]] if j else (), sig=(j == FC - 1))
                        Sx.op("dve", lambda e, m=m, pd=pd, hs=hs: e.scalar_tensor_tensor(
                            out=hres[hs][:], in0=bank(pd), scalar=Gv[:, gbase + m:gbase + m + 1], in1=hres[hs][:],
                            op0=ALU.mult, op1=ALU.add), reads=[psB[pd], bMod], writes=[bHres[hs]])
                        Sx.dma("sp", "hres%d" % hs, lambda e, m=m, t=t, hs=hs: e.dma_start(
                            out=dstr[m, :, t * T:(t + 1) * T], in_=hres[hs][:]),
                            reads=[bHres[hs]], writes=[bH[t]] if m == 0 else (), accw=[bH[t]] if m else ())
                        if m == 1 and t + 1 < NTR:
                            chain_b(t + 1)

        def proj_phase(l):
            i = 1
            with arena_scope() as ph:
                Win = sb("Win", [128, KC, INW], BF16, ph)
                wsT_f = sb("wsT_f", [128, 4, 128], F32, ph)
                wsT_bf = sb("wsT_bf", [128, 4, 128], BF16, ph)
                lng = sb("lng", [128, 512], F32, ph)
                lnb = sb("lnb", [128, 512], F32, ph)
                bsr = sb("bsr", [1, 512], F32, ph)
                hbuf = sb("hbuf", [128, KC, T], F32, ph)
                sq = sb("sq", [128, KC, T], BF16, ph)
                yTs = [sb("yT%d" % j, [128, KC, T], BF16, ph) for j in range(2)]
                tmps = [sb("tmp%d" % j, [128, T], F32, ph) for j in range(2)]
                lnt = sb("lnt", [128, T], F32, ph)
                rstd = sb("rstd", [128, T], F32, ph)
                ug = sb("ug", [128, 4, T], F32, ph)
                vg4 = sb("vg4", [128, 4, 512], F32, ph)
                vsq = [sb("vsq%d" % j, [128, 512], F32, ph) for j in range(2)]
                s1 = sb("s1", [128, 16], F32, ph)
                s2 = sb("s2", [128, 16], F32, ph)
                mean = sb("mean", [128, 16], F32, ph)
                msq = sb("msq", [128, 16], F32, ph)
                var = sb("var", [128, 16], F32, ph)
                lnv = sb("lnv", [128, 16], F32, ph)
                rs = sb("rs", [128, 16], F32, ph)
                vnf = [sb("vnf%d" % j, [128, 512], F32, ph) for j in range(2)]
                vnt = [sb("vnt%d" % j, [128, 512], F32, ph) for j in range(2)]
                vnb = [sb("vnb%d" % j, [128, 512], BF16, ph) for j in range(4)]
                cs = [sb("cs%d" % j, [128, 2, T], F32, ph) for j in range(2)]
                t1 = [sb("t1_%d" % j, [128, T], F32, ph) for j in range(2)]
                t2 = [sb("t2_%d" % j, [128, T], F32, ph) for j in range(2)]
                oa = [sb("oa%d" % j, [128, 4, T], BF16, ph) for j in range(2)]
                qr = [sb("qr%d" % j, [128, 4, T], BF16, ph) for j in range(2)]
                kr = [sb("kr%d" % j, [128, 4, T], BF16, ph) for j in range(2)]
                va = [sb("va%d" % j, [128, 4, 8, 65], BF16, ph) for j in range(2)]
                bWin, bWs, bLnc, bHb, bSq, bLn, bRs, bUg = (NB() for _ in range(8))
                bYs = [[NB() for _ in range(KC)] for _ in range(2)]
                bTmps = [NB() for _ in range(2)]
                bVg = [NB() for _ in range(4)]
                bVsq = [NB() for _ in range(2)]
                bStat = NB()
                bVnf = [NB() for _ in range(2)]
                bVnt = [NB() for _ in range(2)]
                bVnb = [NB() for _ in range(4)]
                bCs = [NB() for _ in range(2)]
                bT1 = [NB() for _ in range(2)]
                bT2 = [NB() for _ in range(2)]
                bOa = [NB() for _ in range(2)]
                bQr = [NB() for _ in range(2)]
                bKr = [NB() for _ in range(2)]
                bVa = [NB() for _ in range(2)]
                load_weight(Win, win_d[l], KC, D, "wg", bWin)
                Sx.dma("sp", "const", lambda e: e.dma_start(out=wsT_f[:], in_=wsT_d[l]), writes=[bLnc])
                Sx.dma("sp", "const", lambda e: e.dma_start(out=lng[:], in_=lng_d[l]), accw=[bLnc])
                Sx.dma("sp", "const", lambda e: e.dma_start(out=lnb[:], in_=lnb_d[l]), accw=[bLnc])
                Sx.dma("sp", "const", lambda e: e.dma_start(out=bsr[:], in_=bs_d[l]), accw=[bLnc])
                tril_b = cust(tril_sb, 0, [[0, 4], [1, 128]])
                Sx.op("dve", lambda e: e.tensor_tensor(out=wsT_bf[:], in0=wsT_f[:], in1=tril_b, op=ALU.mult),
                      reads=[bLnc, bConst], writes=[bWs])
                for j in range(2):
                    Sx.op("pool", lambda e, j=j: e.memset(va[j][:], 1.0), writes=[bVa[j]])
                PST, PZ = 0, 1
                qsb = [sb("qsb%d" % j, [128, T], BF16, ph) for j in range(2)]
                bQsb = [NB() for _ in range(2)]
                qctr = [0]
                work = [2, 3, 4, 5, 6, 7]
                wk = [0]

                def nb_():
                    b = work[wk[0] % 6]
                    wk[0] += 1
                    return b

                def load_h(t):
                    Sx.dma("sp", "h", lambda e, t=t: e.dma_start(out=hbuf[:], in_=hsrc_tile(hS, t)),
                           reads=[bH[t]], writes=[bHb])

                def mm8(pb, lhs_fn, rhs_fn, rd):
                    for kc in range(KC):
                        lhs = lhs_fn(kc)
                        rhs = rhs_fn(kc)
                        Sx.op("pe", lambda e, kc=kc, lhs=lhs, rhs=rhs: e.matmul(
                            bank(pb), lhsT=lhs, rhs=rhs, start=(kc == 0), stop=(kc == KC - 1)),
                            reads=[bWin, cur["bY"][kc]] + rd, writes=[psB[pb]] if kc == 0 else (),
                            accw=[psB[pb]] if kc else (), sig=(kc == KC - 1))

                load_h(0)
                norm_chain_a(hbuf, bHb, sq, bSq, PST)
                norm_chain_rstd(PST, lnt, bLn, rstd, bRs)
                modulate(hbuf, bHb, rstd, bRs, tmps, bTmps, yTs[0], bYs[0], l, i)
                cur = {}
                for t in range(NT):
                    sl = t % 2
                    cur["yT"] = yTs[t % 2]
                    cur["bY"] = bYs[t % 2]
                    yT = yTs[t % 2]
                    Sx.dma("sp", "cs%d" % sl, lambda e, t=t, sl=sl: e.dma_start(
                        out=cs[sl][:], in_=rope_d.rearrange("c p s -> p c s")[:, :, t * T:(t + 1) * T]), writes=[bCs[sl]])
                    if t + 1 < NT:
                        load_h(t + 1)
                    for hd in range(4):
                        pb = nb_()
                        mm8(pb, lambda kc, hd=hd: Win[:, kc, hd * 128:(hd + 1) * 128], lambda kc: yT[:, kc, :], [])
                        Sx.op("act", lambda e, pb=pb, hd=hd: e.activation(out=ug[:, hd, :], in_=bank(pb),
                                                                          func=AF.Gelu_apprx_tanh),
                              reads=[psB[pb]], writes=[bUg] if hd == 0 else (), accw=[bUg] if hd else ())
                    for s in range(4):
                        pb = nb_()
                        mm8(pb, lambda kc, s=s: yT[:, kc, s * 128:(s + 1) * 128], lambda kc: Win[:, kc, 512:1024], [])
                        Sx.op("act", lambda e, pb=pb, s=s: e.activation(out=vg4[:, s, :], in_=bank(pb),
                                                                        func=AF.Gelu_apprx_tanh),
                              reads=[psB[pb]], writes=[bVg[s]])
                        Sx.op("act", lambda e, s=s: e.activation(out=vsq[s % 2][:], in_=vg4[:, s, :], func=AF.Square),
                              reads=[bVg[s]], writes=[bVsq[s % 2]])
                        Sx.op("dve", lambda e, s=s: e.tensor_reduce(
                            out=s1[:, s * 4:(s + 1) * 4], in_=vg4[:, s, :].rearrange("p (h c) -> p h c", h=4),
                            axis=AX.X, op=ALU.add), reads=[bVg[s]], writes=[bStat] if s == 0 else (),
                            accw=[bStat] if s else ())
                        Sx.op("dve", lambda e, s=s: e.tensor_reduce(
                            out=s2[:, s * 4:(s + 1) * 4], in_=vsq[s % 2][:].rearrange("p (h c) -> p h c", h=4),
                            axis=AX.X, op=ALU.add), reads=[bVsq[s % 2]], accw=[bStat])
                    Sx.op("dve", lambda e: e.tensor_scalar(out=mean[:], in0=s1[:], scalar1=1.0 / 128, scalar2=None,
                                                           op0=ALU.mult), reads=[bStat], accw=[bStat])
                    Sx.op("dve", lambda e: e.tensor_tensor(out=msq[:], in0=mean[:], in1=mean[:], op=ALU.mult),
                          reads=[bStat], accw=[bStat])
                    Sx.op("dve", lambda e: e.scalar_tensor_tensor(out=var[:], in0=s2[:], scalar=1.0 / 128, in1=msq[:],
                                                                  op0=ALU.mult, op1=ALU.subtract),
                          reads=[bStat], accw=[bStat])
                    Sx.op("act", lambda e: e.activation(out=lnv[:], in_=var[:], func=AF.Ln, bias=eps_sb[:], scale=1.0),
                          reads=[bStat, bConst], accw=[bStat])
                    Sx.op("act", lambda e: e.activation(out=rs[:], in_=lnv[:], func=AF.Exp, scale=-0.5),
                          reads=[bStat], accw=[bStat])
                    if t + 1 < NT:
                        norm_chain_a(hbuf, bHb, sq, bSq, PST)
                    for s in range(4):
                        x = s % 2
                        for hd in range(4):
                            idx = s * 4 + hd
                            Sx.op("dve", lambda e, s=s, hd=hd, idx=idx, x=x: e.tensor_scalar(
                                out=vnf[x][:, hd * 128:(hd + 1) * 128], in0=vg4[:, s, hd * 128:(hd + 1) * 128],
                                scalar1=mean[:, idx:idx + 1], scalar2=rs[:, idx:idx + 1],
                                op0=ALU.subtract, op1=ALU.mult),
                                reads=[bVg[s], bStat], writes=[bVnf[x]] if hd == 0 else (), accw=[bVnf[x]] if hd else ())
                        Sx.op("pool", lambda e, x=x: e.tensor_tensor(out=vnt[x][:], in0=vnf[x][:], in1=lng[:], op=ALU.mult),
                              reads=[bVnf[x], bLnc], writes=[bVnt[x]])
                        Sx.op("pool", lambda e, x=x, s=s: e.tensor_tensor(out=vnb[s][:], in0=vnt[x][:], in1=lnb[:], op=ALU.add),
                              reads=[bVnt[x], bLnc], writes=[bVnb[s]])
                    for (dstb, bDst, c0, scl) in ((qr, bQr, 1024, 0.125), (kr, bKr, 1536, 1.0)):
                        for c in range(4):
                            pa, pp = nb_(), nb_()
                            mm8(pa, lambda kc, c=c, c0=c0: Win[:, kc, c0 + c * 128:c0 + (c + 1) * 128],
                                lambda kc: yT[:, kc, :], [])
                            qx = qctr[0] % 2
                            qctr[0] += 1
                            tok_cp = Sx.op("act", lambda e, pa=pa, qx=qx: e.activation(out=qsb[qx][:], in_=bank(pa), func=AF.Identity),
                                           reads=[psB[pa]], writes=[bQsb[qx]])
                            Sx.op("pe", lambda e, pp=pp, qx=qx: e.matmul(bank(pp), lhsT=perm_bf[:], rhs=qsb[qx][:],
                                                                        start=True, stop=True),
                                  reads=[bQsb[qx], bConst], writes=[psB[pp]])
                            x = c % 2
                            Sx.op("dve", lambda e, pa=pa, x=x, scl=scl, sl=sl: e.scalar_tensor_tensor(
                                out=t1[x][:], in0=bank(pa), scalar=scl, in1=cs[sl][:, 0, :], op0=ALU.mult, op1=ALU.mult),
                                reads=[psB[pa], bCs[sl]], writes=[bT1[x]], extra=[tok_cp])
                            Sx.op("dve", lambda e, pp=pp, x=x, scl=scl, sl=sl: e.scalar_tensor_tensor(
                                out=t2[x][:], in0=bank(pp), scalar=scl, in1=cs[sl][:, 1, :], op0=ALU.mult, op1=ALU.mult),
                                reads=[psB[pp], bCs[sl]], writes=[bT2[x]])
                            Sx.op("pool", lambda e, x=x, c=c, dstb=dstb, sl=sl: e.tensor_tensor(
                                out=dstb[sl][:, c, :], in0=t1[x][:], in1=t2[x][:], op=ALU.add),
                                reads=[bT1[x], bT2[x]], writes=[bDst[sl]] if c == 0 else (), accw=[bDst[sl]] if c else ())
                    Sx.dma("sp", "qr%d" % sl, lambda e, t=t, sl=sl: e.dma_start(
                        out=qS.rearrange("(c p) s -> p c s", p=128)[:, :, t * T:(t + 1) * T], in_=qr[sl][:]),
                        reads=[bQr[sl]], writes=[bQd] if t == 0 else (), accw=[bQd] if t else ())
                    Sx.dma("sp", "kr%d" % sl, lambda e, t=t, sl=sl: e.dma_start(
                        out=kS.rearrange("(c p) s -> p c s", p=128)[:, :, t * T:(t + 1) * T], in_=kr[sl][:]),
                        reads=[bKr[sl]], writes=[bKd] if t == 0 else (), accw=[bKd] if t else ())
                    if t + 1 < NT:
                        norm_chain_rstd(PST, lnt, bLn, rstd, bRs)
                        modulate(hbuf, bHb, rstd, bRs, tmps, bTmps, yTs[(t + 1) % 2], bYs[(t + 1) % 2], l, i)

                    for s in range(4):
                        pb = nb_()
                        mm8(pb, lambda kc, s=s: yT[:, kc, s * 128:(s + 1) * 128], lambda kc: Win[:, kc, 2048:2560], [])
                        Sx.op("act", lambda e, pb=pb, s=s, sl=sl: e.activation(
                            out=va[sl][:, s, :, 0:64], in_=bank(pb).rearrange("p (h c) -> p h c", h=8), func=AF.Identity),
                            reads=[psB[pb]], accw=[bVa[sl]], extra=list(((k[0], k[1], n) for k, n in bVa[sl].r.items())))
                    Sx.dma("sp", "va%d" % sl, lambda e, t=t, sl=sl: e.dma_start(
                        out=vS[t * T:(t + 1) * T, :].rearrange("(s p) c -> p s c", p=128),
                        in_=va[sl][:].rearrange("p s h c -> p s (h c)")),
                        reads=[bVa[sl]], writes=[bVd] if t == 0 else (), accw=[bVd] if t else ())
                    for s in range(4):
                        for hd in range(4):
                            Sx.op("pe", lambda e, s=s, hd=hd: e.matmul(
                                ps[:, PZ, hd * 128:(hd + 1) * 128], lhsT=vnb[s][:, hd * 128:(hd + 1) * 128],
                                rhs=wsT_bf[:, hd, :], start=True, stop=False),
                                reads=[bVnb[s], bWs], writes=[psB[PZ]] if hd == 0 else (), accw=[psB[PZ]] if hd else (),
                                sig=False)
                            Sx.op("pe", lambda e, s=s, hd=hd: e.matmul(
                                ps[:, PZ, hd * 128:(hd + 1) * 128], lhsT=ones_row[0:1, :],
                                rhs=bsr[0:1, hd * 128:(hd + 1) * 128], start=False, stop=True),
                                reads=[bLnc, bConst], accw=[psB[PZ]], sig=(hd == 3))
                        Sx.op("dve", lambda e, s=s, sl=sl: e.tensor_tensor(
                            out=oa[sl][:, :, s * 128:(s + 1) * 128],
                            in0=ps[:, PZ, :].rearrange("p (h c) -> p h c", h=4),
                            in1=ug[:, :, s * 128:(s + 1) * 128], op=ALU.mult),
                            reads=[psB[PZ], bUg], writes=[bOa[sl]] if s == 0 else (), accw=[bOa[sl]] if s else ())
                    Sx.dma("sp", "oa%d" % sl, lambda e, t=t, sl=sl: e.dma_start(
                        out=mS[0:512, :].rearrange("(c p) s -> p c s", p=128)[:, :, t * T:(t + 1) * T], in_=oa[sl][:]),
                        reads=[bOa[sl]], writes=[bMd[t]])
        def attn_phase(l):
            with arena_scope() as ph:
                Qs = [sb("Qs%d" % j, [128, S], BF16, ph) for j in range(2)]
                Ks = [sb("Ks%d" % j, [128, S], BF16, ph) for j in range(2)]
                VR = {r: sb("VR%d" % r, [128, 32, 520], BF16, ph) for r in (1, 4, 16)}
                acc = [sb("acc%d" % j, [128, S], F32, ph) for j in range(2)]
                pT = [sb("pT%d" % j, [128, 2, 4, 128], BF16, ph) for j in range(3)]
                ob = [sb("ob%d" % j, [128, S], BF16, ph) for j in range(2)]
                sm = [sb("sm%d" % j, [128, 2, 512], F32, ph) for j in range(3)]
                rb = [sb("rb%d" % j, [64, 512], F32, ph) for j in range(1)] * 2
                bSm = [[NB(), NB()] for _ in range(3)]
                bRb = [NB()] * 2
                bQ = [NB() for _ in range(2)]
                bK = [NB() for _ in range(2)]
                bVR = {r: NB() for r in (1, 4, 16)}
                bAcc = [NB() for _ in range(2)]
                bPT = [[NB(), NB()] for _ in range(3)]
                bOb = [NB() for _ in range(2)]
                mb_bf = sb("mb_bf", [128, 4, 512], BF16, ph)
                bMb = NB()
                mb_f = cust(sm[0], 0, [[512, 4], [1, 512]])
                Sx.dma("sp", "const", lambda e: e.dma_start(out=mb_f, in_=mb_d.rearrange("v p f -> p v f")), writes=[bSm[0][0]],
                       accw=[bSm[0][1], bSm[1][0], bSm[1][1]])
                Sx.op("dve", lambda e: e.tensor_copy(out=mb_bf[:], in_=mb_f), reads=[bSm[0][0], bSm[0][1], bSm[1][0], bSm[1][1]], writes=[bMb])
                for r in (1, 4, 16):
                    nbk = 32 // r
                    for rho in range(r):
                        srcv = bass.AP(vS.tensor, rho * 520, [[r * 520, 128], [128 * r * 520, nbk], [1, 520]])
                        Sx.dma("sp", "vr%d" % r, lambda e, r=r, rho=rho, nbk=nbk, srcv=srcv: e.dma_start(
                            out=VR[r][:, rho * nbk:(rho + 1) * nbk, :], in_=srcv),
                            reads=[bVd], writes=[bVR[r]] if rho == 0 else (), accw=[bVR[r]] if rho else ())
                PSS = ((0, 1), (2, 3), (4, 5))
                PSO = (6, 7)
                PSN = (6, 7)
                octr = [0]
                nctr = [0]

                def load_qk(head):
                    sl = head % 2
                    for (dst, srcS, bD, bS_) in ((Qs, qS, bQ, bQd), (Ks, kS, bK, bKd)):
                        for dup in range(2):
                            Sx.dma("sp", "qk%d" % sl, lambda e, head=head, sl=sl, dup=dup, dst=dst, srcS=srcS: e.dma_start(
                                out=dst[sl][dup * 64:(dup + 1) * 64, :], in_=srcS[head * 64:(head + 1) * 64, :]),
                                reads=[bS_], writes=[bD[sl]] if dup == 0 else (), accw=[bD[sl]] if dup else ())

                groups = []
                for head in range(8):
                    for r in (1, 4, 16):
                        for g in range(8):
                            groups.append((head, r, g))

                def blocks_of(r, g):
                    nbk = 32 // r
                    out = []
                    for b in range(4):
                        bi = 4 * g + b
                        rho, n = divmod(bi, nbk)
                        out.append((bi, rho, n))
                    return out

                def emit_qk(i):
                    head, r, g = groups[i]
                    sl = head % 2
                    pss = PSS[i % 3]
                    px = i % 3
                    blocks = blocks_of(r, g)
                    pP, pC = pss
                    for b in range(4):
                        bi, rho, n = blocks[b]
                        q0 = rho + r * 128 * n
                        p0 = rho + r * 128 * (n - 1) if n > 0 else q0
                        qsl = slice(q0, q0 + r * 127 + 1, r)
                        psl = slice(p0, p0 + r * 127 + 1, r)
                        Sx.op("pe", lambda e, pP=pP, b=b, psl=psl, qsl=qsl, sl=sl: e.matmul(
                            ps[:, pP, b * 128:(b + 1) * 128], lhsT=Ks[sl][0:64, psl], rhs=Qs[sl][0:64, qsl],
                            start=True, stop=True),
                            reads=[bQ[sl], bK[sl]], writes=[psB[pP]] if b == 0 else (), accw=[psB[pP]] if b else (),
                            sig=False)
                        Sx.op("pe", lambda e, pC=pC, b=b, qsl=qsl, sl=sl: e.matmul(
                            ps[:, pC, b * 128:(b + 1) * 128], lhsT=Ks[sl][64:128, qsl], rhs=Qs[sl][64:128, qsl],
                            start=True, stop=True),
                            reads=[bQ[sl], bK[sl]], writes=[psB[pC]] if b == 0 else (), accw=[psB[pC]] if b else (),
                            sig=(b == 3))
                    if r == 16:
                        vP = 2
                    else:
                        vP = 0 if blocks[0][2] == 0 else 1
                    for half, (pbk, variant) in enumerate(((pP, vP), (pC, 3))):
                        Sx.op("dve", lambda e, pbk=pbk, px=px, half=half, variant=variant: e.tensor_tensor(
                            out=sm[px][:, half, :], in0=bank(pbk), in1=mb_bf[:, variant, :], op=ALU.add),
                            reads=[psB[pbk], bMb], writes=[bSm[px][half]])
                        Sx.op("act", lambda e, px=px, half=half: e.activation(
                            out=pT[px][:, half, :, :].rearrange("p b c -> p (b c)"),
                            in_=sm[px][:, half, :], func=AF.Exp), reads=[bSm[px][half]], writes=[bPT[px][half]])

                def emit_pv(i):
                    head, r, g = groups[i]
                    a = head % 2
                    pso = PSO[octr[0] % 2]
                    octr[0] += 1
                    px = i % 3
                    blocks = blocks_of(r, g)
                    for b in range(4):
                        bi, rho, n = blocks[b]
                        first = (b == 0)
                        if n > 0:
                            Sx.op("pe", lambda e, pso=pso, b=b, bi=bi, head=head, px=px, r=r: e.matmul(
                                ps[0:65, pso, b * 128:(b + 1) * 128],
                                lhsT=VR[r][:, bi - 1, head * 65:(head + 1) * 65], rhs=pT[px][:, 0, b, :],
                                start=True, stop=False),
                                reads=[bVR[r], bPT[px][0]], writes=[psB[pso]] if first else (),
                                accw=() if first else [psB[pso]], sig=False)
                            first = False
                        Sx.op("pe", lambda e, pso=pso, b=b, bi=bi, head=head, px=px, r=r, n=n: e.matmul(
                            ps[0:65, pso, b * 128:(b + 1) * 128],
                            lhsT=VR[r][:, bi, head * 65:(head + 1) * 65], rhs=pT[px][:, 1, b, :],
                            start=(n == 0), stop=True),
                            reads=[bVR[r], bPT[px][1]], writes=[psB[pso]] if first else (),
                            accw=() if first else [psB[pso]], sig=(b == 3))
                    if r == 1:
                        Sx.op("act", lambda e, pso=pso, g=g, a=a: e.activation(
                            out=acc[a][0:65, g * 512:(g + 1) * 512], in_=ps[0:65, pso, :], func=AF.Identity),
                            reads=[psB[pso]], writes=[bAcc[a]] if g == 0 else (), accw=[bAcc[a]] if g else ())
                    elif r == 4:
                        rho = g // 2
                        o0 = rho + 2048 * (g % 2)
                        Sx.op("dve", lambda e, pso=pso, o0=o0, a=a: e.tensor_tensor(
                            out=acc[a][0:65, o0:o0 + 2045:4], in0=ps[0:65, pso, :],
                            in1=acc[a][0:65, o0:o0 + 2045:4], op=ALU.add),
                            reads=[psB[pso], bAcc[a]], accw=[bAcc[a]])
                    else:
                        av = cust(acc[a], 2 * g, [[1, 2], [16, 256]], nparts=65)
                        Sx.op("dve", lambda e, pso=pso, av=av: e.tensor_tensor(
                            out=av, in0=ps[0:65, pso, :].rearrange("p (a j) -> p a j", a=2),
                            in1=av, op=ALU.add),
                            reads=[psB[pso], bAcc[a]], accw=[bAcc[a]])

                def emit_norm(head):
                    a = head % 2
                    for sli in range(8):
                        pn = PSO[octr[0] % 2]
                        octr[0] += 1
                        rx = nctr[0] % 2
                        nctr[0] += 1
                        Sx.op("pe", lambda e, pn=pn, sli=sli, a=a: e.matmul(
                            ps[0:64, pn, :], lhsT=sel_sb[0:65, :], rhs=acc[a][0:65, sli * 512:(sli + 1) * 512],
                            start=True, stop=True), reads=[bAcc[a], bConst], writes=[psB[pn]])
                        Sx.op("act", lambda e, pn=pn, rx=rx: e.activation(out=rb[rx][:], in_=ps[0:64, pn, :], func=AF.Ln),
                              reads=[psB[pn]], writes=[bRb[rx]])
                        Sx.op("act", lambda e, rx=rx: e.activation(out=rb[rx][:], in_=rb[rx][:], func=AF.Exp, scale=-1.0),
                              reads=[bRb[rx]], writes=[bRb[rx]])
                        Sx.op("pool", lambda e, sli=sli, a=a, rx=rx: e.tensor_tensor(
                            out=ob[a][0:64, sli * 512:(sli + 1) * 512], in0=rb[rx][:],
                            in1=acc[a][0:64, sli * 512:(sli + 1) * 512], op=ALU.mult),
                            reads=[bRb[rx], bAcc[a]], writes=[bOb[a]] if sli == 0 else (), accw=[bOb[a]] if sli else ())
                    Sx.dma("sp", "ob%d" % a, lambda e, head=head, a=a: e.dma_start(
                        out=mS[512 + head * 64:512 + (head + 1) * 64, :], in_=ob[a][0:64, :]),
                        reads=[bOb[a]], accw=bMd)

                NG = len(groups)
                load_qk(0)
                emit_qk(0)
                emit_qk(1)
                pending_norm = None
                for i in range(NG):
                    head, r, g = groups[i]
                    if i + 2 < NG:
                        emit_qk(i + 2)
                    if r == 1 and g == 0 and head + 1 < 8:
                        load_qk(head + 1)
                    emit_pv(i)
                    if pending_norm is not None and r == 1 and g == 2:
                        emit_norm(pending_norm)
                        pending_norm = None
                    if r == 16 and g == 7:
                        pending_norm = head
                emit_norm(pending_norm)

        def outproj_phase(l):
            with arena_scope() as ph:
                Wo = sb("Wo", [128, KC, D], BF16, ph)
                mx = [sb("mx%d" % j, [128, KC, T], BF16, ph) for j in range(2)]
                NHR = 6
                hres = [sb("hres%d" % j, [128, 2, T], F32, ph) for j in range(NHR)]
                bWo = NB()
                bMx = [NB() for _ in range(2)]
                bHres = [NB() for _ in range(NHR)]
                load_weight(Wo, wout_d[l], KC, D, "wo", bWo)
                hrp = hS.rearrange("(k p) s -> p k s", p=128)
                gbase = (l * 3 + 1) * KC

                def load_mx(t):
                    sl = t % 2
                    Sx.dma("sp", "mx%d" % sl, lambda e, t=t, sl=sl: e.dma_start(
                        out=mx[sl][:], in_=mS.rearrange("(k p) s -> p k s", p=128)[:, :, t * T:(t + 1) * T]),
                        reads=[bMd[t]], writes=[bMx[sl]])

                pairs = [(t, mp) for t in range(NT) for mp in range(KC // 2)]

                def load_pair(pi):
                    t, mp = pairs[pi]
                    hs = pi % NHR
                    Sx.dma("sp", "hro%d" % hs, lambda e, mp=mp, t=t, hs=hs: e.dma_start(
                        out=hres[hs][:], in_=hrp[:, 2 * mp:2 * mp + 2, t * T:(t + 1) * T]), reads=[bH[t]], writes=[bHres[hs]])

                load_mx(0)
                PF = 4
                for pi in range(PF):
                    load_pair(pi)
                ci = 0
                for pi, (t, mp) in enumerate(pairs):
                    sl = t % 2
                    if mp == 0 and t + 1 < NT:
                        load_mx(t + 1)
                    if pi + PF < len(pairs):
                        load_pair(pi + PF)
                    hs = pi % NHR
                    for sub in range(2):
                        m = 2 * mp + sub
                        pd = ci % 4
                        ci += 1
                        for kc in range(KC):
                            Sx.op("pe", lambda e, kc=kc, m=m, pd=pd, sl=sl: e.matmul(
                                bank(pd), lhsT=Wo[:, kc, m * 128:(m + 1) * 128], rhs=mx[sl][:, kc, :],
                                start=(kc == 0), stop=(kc == KC - 1)),
                                reads=[bWo, bMx[sl]], writes=[psB[pd]] if kc == 0 else (),
                                accw=[psB[pd]] if kc else (), sig=(kc == KC - 1))
                        Sx.op("dve", lambda e, m=m, pd=pd, hs=hs, sub=sub: e.scalar_tensor_tensor(
                            out=hres[hs][:, sub, :], in0=bank(pd), scalar=Gv[:, gbase + m:gbase + m + 1],
                            in1=hres[hs][:, sub, :], op0=ALU.mult, op1=ALU.add),
                            reads=[psB[pd], bMod, bHres[hs]], accw=[bHres[hs]])
                    Sx.dma("act", "hro%d" % hs, lambda e, mp=mp, t=t, hs=hs: e.dma_start(
                        out=hrp[:, 2 * mp:2 * mp + 2, t * T:(t + 1) * T], in_=hres[hs][:]),
                        reads=[bHres[hs]], accw=[bH[t]])

        def final_phase():
            with arena_scope() as ph:
                hb = [sb("hb%d" % j, [128, KC, T], F32, ph) for j in range(2)]
                sq = sb("sq", [128, KC, T], BF16, ph)
                lnt = sb("lnt", [128, T], F32, ph)
                rstd = sb("rstd", [128, T], F32, ph)
                ot = [sb("ot%d" % j, [128, KC, T], F32, ph) for j in range(2)]
                bHb = [NB() for _ in range(2)]
                bSq, bLn, bRs = NB(), NB(), NB()
                bOt = [NB() for _ in range(2)]

                def load_h(t):
                    Sx.dma("sp", "hb%d" % (t % 2), lambda e, t=t: e.dma_start(out=hb[t % 2][:], in_=hsrc_tile(hS, t)),
                           reads=[bH[t]], writes=[bHb[t % 2]])

                load_h(0)
                for t in range(NT):
                    sl = t % 2
                    if t + 1 < NT:
                        load_h(t + 1)
                    norm_chain_a(hb[sl], bHb[sl], sq, bSq, 0)
                    norm_chain_rstd(0, lnt, bLn, rstd, bRs)
                    for kc in range(KC):
                        Sx.op("dve", lambda e, kc=kc, sl=sl: e.scalar_tensor_tensor(
                            out=ot[sl][:, kc, :], in0=hb[sl][:, kc, :], scalar=fg_sb[:, kc:kc + 1], in1=rstd[:],
                            op0=ALU.mult, op1=ALU.mult), reads=[bHb[sl], bRs, bConst],
                            writes=[bOt[sl]] if kc == 0 else (), accw=[bOt[sl]] if kc else ())
                    out_toks.append(Sx.dma("sp", "ot%d" % sl, lambda e, t=t, sl=sl: e.dma_start(
                        out=outT.rearrange("(k p) s -> p k s", p=128)[:, :, t * T:(t + 1) * T], in_=ot[sl][:]),
                        reads=[bOt[sl]]))

        phases = []
        for l in range(DEPTH):
            phases.append(("ffn1_%d" % l, lambda l=l: ffn_phase(l, 0, xT if l == 0 else hS)))
            phases.append(("proj_%d" % l, lambda l=l: proj_phase(l)))
            phases.append(("attn_%d" % l, lambda l=l: attn_phase(l)))
            phases.append(("ffn2_%d" % l, lambda l=l: ffn_phase(l, 1, hS, pre=lambda: outproj_phase(l))))
        phases.append(("final", final_phase))
        if stop_after == "ada":
            phases = []
        for name, fn in phases:
            fn()
            if stop_after == name:
                break

        final = list(out_toks)
        for b in bH + bMd + [bQd, bKd, bVd]:
            for k, n in b.w.items():
                final.append((k[0], k[1], n))
        Sx.wait_all("sp", final)
        ok_, stuck_, val_ = Sx.check_deadlock()
        if not ok_:
            raise RuntimeError("semaphore deadlock in generated program: %r" % (stuck_,))
        Sx.run(block)
    return nc


def _perm_half(w):
    d, n = w.shape
    w4 = w.reshape(d, n // 64, 2, 32)
    return np.ascontiguousarray(w4[:, :, ::-1, :]).reshape(d, n)


def _consts():
    pos = np.arange(S, dtype=np.float32)
    inv = (np.float32(10000.0) ** (-np.arange(0, 64, 2, dtype=np.float32) / np.float32(64))).astype(np.float32)
    ang = (pos[:, None] * inv[None, :]).astype(np.float32)
    ang = np.concatenate([ang, ang], axis=-1)
    cos = np.cos(ang).astype(np.float32).T
    sin = np.sin(ang).astype(np.float32).T
    ssin = sin.copy()
    ssin[:32] *= -1.0
    rope = np.stack([np.concatenate([cos, cos], 0), np.concatenate([ssin, ssin], 0)], 0).astype(np.float32)
    k = np.arange(128)[:, None]
    i = np.arange(128)[None, :]
    tril = (k <= i).astype(np.float32)
    prev = np.where(k >= i, 0.0, NEG).astype(np.float32)
    cur = np.where(k <= i, 0.0, NEG).astype(np.float32)
    dead = np.full((128, 128), NEG, np.float32)
    mb = np.stack([np.concatenate([dead, prev, prev, prev], 1),
                   np.concatenate([prev, prev, prev, prev], 1),
                   np.concatenate([dead, prev, dead, prev], 1),
                   np.concatenate([cur, cur, cur, cur], 1)], 0).astype(np.float32)
    ident = np.eye(128, dtype=np.float32)
    sel = np.zeros((128, 64), np.float32)
    sel[64, :] = 1.0
    m_ = np.arange(128)
    partner = (m_ // 64) * 64 + np.where(m_ % 64 < 32, m_ % 64 + 32, m_ % 64 - 32)
    permh = np.zeros((128, 128), np.float32)
    permh[partner, m_] = 1.0
    return rope, tril, mb, ident, sel, permh


def _prep_inputs(x, c, ada_w, ada_b, norm_g, ffn1_wg, ffn1_wu, ffn1_wd, ffn2_wg, ffn2_wu, ffn2_wd,
                 w_in, sgu_ln_g, sgu_ln_b, sgu_w, sgu_b, w_out, final_g):
    f = lambda a: np.ascontiguousarray(np.asarray(a, dtype=np.float32))
    x, c, ada_w, ada_b, norm_g = f(x), f(c), f(ada_w), f(ada_b), f(norm_g)
    w_in = f(w_in)
    rope, tril, mb, ident, sel, permh = _consts()
    shared = {
        "ada_w": ada_w,
        "ada_b": np.ascontiguousarray(ada_b.reshape(DEPTH, 72, 128).transpose(2, 0, 1).reshape(128, DEPTH * 72)),
        "ngv": np.ascontiguousarray(norm_g.reshape(DEPTH, 3, KC, 128).transpose(3, 0, 1, 2).reshape(128, DEPTH * 3 * KC)),
        "fgv": np.ascontiguousarray(f(final_g).reshape(KC, 128).T),
        "ffn1_wg": f(ffn1_wg), "ffn1_wu": f(ffn1_wu), "ffn1_wd": f(ffn1_wd),
        "ffn2_wg": f(ffn2_wg), "ffn2_wu": f(ffn2_wu), "ffn2_wd": f(ffn2_wd),
        "w_in": w_in,
        "wsT": np.ascontiguousarray(f(sgu_w).transpose(0, 3, 1, 2)),
        "lng_b": np.ascontiguousarray(np.broadcast_to(f(sgu_ln_g).reshape(DEPTH, 1, 512), (DEPTH, 128, 512))),
        "lnb_b": np.ascontiguousarray(np.broadcast_to(f(sgu_ln_b).reshape(DEPTH, 1, 512), (DEPTH, 128, 512))),
        "bs_row": np.ascontiguousarray(f(sgu_b).reshape(DEPTH, 1, 512)),
        "w_out": f(w_out),
        "rope": rope, "tril": tril, "mbias": mb, "ident": ident, "sel65": sel, "permh": permh,
    }
    in_maps = []
    for b in range(x.shape[0]):
        m = dict(shared)
        m["xT"] = np.ascontiguousarray(x[b].T)
        m["cv"] = np.ascontiguousarray(c[b].reshape(KC, 128).T)
        in_maps.append(m)
    return in_maps


def kernel(x, c, ada_w, ada_b, norm_g, ffn1_wg, ffn1_wu, ffn1_wd, ffn2_wg, ffn2_wu, ffn2_wd,
           w_in, sgu_ln_g, sgu_ln_b, sgu_w, sgu_b, w_out, final_g):
    in_maps = _prep_inputs(x, c, ada_w, ada_b, norm_g, ffn1_wg, ffn1_wu, ffn1_wd, ffn2_wg, ffn2_wu, ffn2_wd,
                           w_in, sgu_ln_g, sgu_ln_b, sgu_w, sgu_b, w_out, final_g)
    nc = build_nc()
    res = run_bass_kernel_spmd(nc, in_maps, core_ids=list(range(8)))
    out = np.stack([np.ascontiguousarray(r["outT"].T) for r in res.results], axis=0)
    return out.astype(np.float32)
```

```python
import numpy as np
from contextlib import ExitStack
import concourse.bass as bass
import concourse.mybir as mybir
from concourse.bass_utils import run_bass_kernel_spmd

F32 = mybir.dt.float32
BF16 = mybir.dt.bfloat16
AF = mybir.ActivationFunctionType
ALU = mybir.AluOpType
AX = mybir.AxisListType

D = 1024
S = 4096
DEPTH = 2
DFF = 2816
KC = 8
FC = 22
T = 512
NT = S // T
NTR = NT
INW = 2560
NADA = 9
EPS = 1e-6
NEG = -30000.0


class Buf:
    __slots__ = ("w", "r", "name")

    def __init__(self, name="", init=None):
        self.w = {}
        self.r = dict(init) if init else {}
        self.name = name


def _merge(d, tok):
    k = (tok[0], tok[1])
    if d.get(k, 0) < tok[2]:
        d[k] = tok[2]


class Sched:
    ENG = ["pe", "act", "dve", "pool", "sp"]

    def __init__(self, nc, stack):
        self.nc = nc
        self.stack = stack
        self.q = {e: [] for e in self.ENG}
        self.sem = {e: stack.enter_context(nc.semaphore("prog_" + e)) for e in self.ENG}
        self.cnt = {e: 0 for e in self.ENG}
        self.seen = {e: {} for e in self.ENG}
        self.dsem = {}
        self.dcnt = {}
        self.nwait = 0
        self.sym = {e: [] for e in self.ENG}

    def _waits(self, eng, deps):
        seen = self.seen[eng]
        for (kind, key), n in deps.items():
            if kind == "e" and key == eng and eng == "pe":
                continue
            if seen.get((kind, key), 0) >= n:
                continue
            seen[(kind, key)] = n
            s = self.sem[key] if kind == "e" else self.dsem[key]
            self.q[eng].append(lambda e, s=s, n=n: e.wait_ge(s, n))
            self.sym[eng].append(("w", (kind, key), n))
            self.nwait += 1

    def _deps(self, reads, writes, accw, extra):
        deps = {}
        for t in extra:
            if t is not None:
                _merge(deps, t)
        for b in reads:
            for k, n in b.w.items():
                _merge(deps, (k[0], k[1], n))
        for b in writes:
            for k, n in b.w.items():
                _merge(deps, (k[0], k[1], n))
            for k, n in b.r.items():
                _merge(deps, (k[0], k[1], n))
        for b in accw:
            for k, n in b.r.items():
                _merge(deps, (k[0], k[1], n))
        return deps

    def _mark(self, tok, reads, writes, accw):
        for b in reads:
            _merge(b.r, tok)
        for b in writes:
            b.w = {(tok[0], tok[1]): tok[2]}
            b.r = {}
        for b in accw:
            _merge(b.w, tok)

    def op(self, eng, fn, reads=(), writes=(), accw=(), sig=True, extra=()):
        deps = self._deps(reads, writes, accw, extra)
        self._waits(eng, deps)
        if sig:
            self.cnt[eng] += 1
            s = self.sem[eng]
            self.q[eng].append(lambda e, fn=fn, s=s: fn(e).then_inc(s, 1))
            self.sym[eng].append(("i", ("e", eng), 1))
            tok = ("e", eng, self.cnt[eng])
        else:
            self.q[eng].append(fn)
            tok = ("e", eng, self.cnt[eng] + 1)
        self._mark(tok, reads, writes, accw)
        return tok

    def dma(self, eng, key, fn, reads=(), writes=(), accw=(), extra=()):
        if key not in self.dsem:
            self.dsem[key] = self.stack.enter_context(self.nc.semaphore("d_" + key))
            self.dcnt[key] = 0
        deps = self._deps(reads, writes, accw, extra)
        self._waits(eng, deps)
        self.dcnt[key] += 16
        s = self.dsem[key]
        self.q[eng].append(lambda e, fn=fn, s=s: fn(e).then_inc(s, 16))
        self.sym[eng].append(("i", ("d", key), 16))
        tok = ("d", key, self.dcnt[key])
        self._mark(tok, reads, writes, accw)
        return tok

    def check_deadlock(self):
        val = {}
        pc = {e: 0 for e in self.ENG}
        progress = True
        while progress:
            progress = False
            for e in self.ENG:
                q = self.sym[e]
                while pc[e] < len(q):
                    kind, key, n = q[pc[e]]
                    if kind == "w":
                        if val.get(key, 0) >= n:
                            pc[e] += 1
                            progress = True
                        else:
                            break
                    else:
                        val[key] = val.get(key, 0) + n
                        pc[e] += 1
                        progress = True
        stuck = {e: (pc[e], len(self.sym[e]), self.sym[e][pc[e]] if pc[e] < len(self.sym[e]) else None) for e in self.ENG}
        ok = all(pc[e] == len(self.sym[e]) for e in self.ENG)
        return ok, stuck, val

    def fence(self):
        f = {}
        for e in self.ENG:
            if self.cnt[e]:
                f[("e", e)] = self.cnt[e]
        for k, n in self.dcnt.items():
            if n:
                f[("d", k)] = n
        return f

    def wait_all(self, eng, toks):
        deps = {}
        for t in toks:
            _merge(deps, t)
        self._waits(eng, deps)

    def run(self, block):
        q = self.q

        @block.tensor
        def _(e):
            for f in q["pe"]:
                f(e)

        @block.scalar
        def _(e):
            for f in q["act"]:
                f(e)

        @block.vector
        def _(e):
            for f in q["dve"]:
                f(e)

        @block.gpsimd
        def _(e):
            for f in q["pool"]:
                f(e)

        @block.sync
        def _(e):
            for f in q["sp"]:
                f(e)


def build_nc(stop_after=None, debug=False):
    nc = bass.Bass("TRN2", target_bir_lowering=False)

    def din(name, shape, dt=F32):
        return nc.dram_tensor(name, list(shape), dt, kind="ExternalInput").ap()

    xT = din("xT", [D, S])
    cv = din("cv", [128, KC])
    ada_w = din("ada_w", [DEPTH, D, NADA * D])
    ada_b = din("ada_b", [128, DEPTH * 72])
    ngv = din("ngv", [128, DEPTH * 3 * KC])
    fgv = din("fgv", [128, KC])
    wg_d = [din("ffn1_wg", [DEPTH, D, DFF]), din("ffn2_wg", [DEPTH, D, DFF])]
    wu_d = [din("ffn1_wu", [DEPTH, D, DFF]), din("ffn2_wu", [DEPTH, D, DFF])]
    wd_d = [din("ffn1_wd", [DEPTH, DFF, D]), din("ffn2_wd", [DEPTH, DFF, D])]
    win_d = din("w_in", [DEPTH, D, INW])
    wsT_d = din("wsT", [DEPTH, 128, 4, 128])
    lng_d = din("lng_b", [DEPTH, 128, 512])
    lnb_d = din("lnb_b", [DEPTH, 128, 512])
    bs_d = din("bs_row", [DEPTH, 1, 512])
    wout_d = din("w_out", [DEPTH, D, D])
    rope_d = din("rope", [2, 128, S])
    tril_d = din("tril", [128, 128])
    mb_d = din("mbias", [4, 128, 512])
    ident_d = din("ident", [128, 128])
    sel_d = din("sel65", [128, 64])
    perm_d = din("permh", [128, 128])

    outT = nc.dram_tensor("outT", [D, S], F32, kind="ExternalOutput").ap()
    dk = "ExternalOutput" if debug else None

    def dscr(name, shape, dt):
        if debug:
            return nc.dram_tensor(name, list(shape), dt, kind="ExternalOutput").ap()
        return nc.dram_tensor(name, list(shape), dt).ap()

    hS = dscr("hS", [D, S], F32)
    qS = dscr("qS", [512, S], BF16)
    kS = dscr("kS", [512, S], BF16)
    vS = dscr("vS", [S, 520], BF16)
    mS = dscr("mS", [D, S], BF16)
    if debug:
        dbg_mod = nc.dram_tensor("dbg_mod", [128, DEPTH * 72], F32, kind="ExternalOutput").ap()

    AW = 53000
    with ExitStack() as st:
        Sx = Sched(nc, st)
        arena = st.enter_context(nc.sbuf_tensor("arena", [128, AW], F32))
        top = [0]
        hiw = [0]
        cur_fence = [None]

        def sb(name, shape, dt, stack=None):
            nel = 1
            for d_ in shape[1:]:
                nel *= d_
            nw = (nel * (4 if dt == F32 else 2) + 3) // 4
            nw = (nw + 7) // 8 * 8
            off = top[0]
            top[0] += nw
            hiw[0] = max(hiw[0], top[0])
            assert top[0] <= AW, ("SBUF arena overflow", name, top[0])
            a = arena[0:shape[0], off:off + nw]
            if dt != F32:
                a = a.bitcast(dt)
            a = a[:, 0:nel]
            if len(shape) > 2:
                names = " ".join("d%d" % i_ for i_ in range(1, len(shape)))
                kw = {"d%d" % i_: shape[i_] for i_ in range(1, len(shape) - 1)}
                a = a.rearrange("p (%s) -> p %s" % (names, names), **kw)
            return a

        class arena_scope:
            def __enter__(self):
                self.mark = top[0]
                return self

            def __exit__(self, *a):
                top[0] = self.mark
                cur_fence[0] = Sx.fence()
                return False

        def NB(name=""):
            return Buf(name, init=cur_fence[0])

        def cust(ap, rel, free, nparts=128):
            return bass.AP(ap.tensor, ap.offset + rel, [[ap.ap[0][0], nparts]] + [list(x) for x in free])

        modv = sb("modv", [128, DEPTH * 72], F32)
        Av = sb("Av", [128, DEPTH * 3 * KC], F32)
        Gv = sb("Gv", [128, DEPTH * 3 * KC], F32)
        ng_sb = sb("ng_sb", [128, DEPTH * 3 * KC], F32)
        fg_sb = sb("fg_sb", [128, KC], F32)
        adab_sb = sb("adab_sb", [128, DEPTH * 72], F32)
        c_sb = sb("c_sb", [128, KC], F32)
        cact = sb("cact", [128, KC], BF16)
        ones_bf = sb("ones_bf", [128, 128], BF16)
        ident_bf = sb("ident_bf", [128, 128], BF16)
        sel_sb = sb("sel_sb", [128, 64], F32)
        perm_bf = sb("perm_bf", [128, 128], BF16)
        ones_row = sb("ones_row", [1, 128], F32)
        eps_sb = sb("eps_sb", [128, 1], F32)
        tril_sb = sb("tril_sb", [128, 128], F32)
        ps = st.enter_context(nc.psum_tensor("ps", [128, 8, 512], F32))
        psB = [Buf("ps%d" % i) for i in range(8)]
        block = st.enter_context(nc.Block())

        bConst = Buf("const")
        bMod = Buf("mod")
        bH = [Buf("H%d" % t) for t in range(NT)]
        bQd, bKd, bVd = Buf("qS"), Buf("kS"), Buf("vS")
        bMd = [Buf("mS%d" % t) for t in range(NT)]
        out_toks = []

        with arena_scope():
            ident_f = sb("ident_f", [128, 128], F32)
            perm_f = sb("perm_f", [128, 128], F32)
            for dst, src in ((ng_sb, ngv), (fg_sb, fgv), (adab_sb, ada_b), (c_sb, cv), (ident_f, ident_d),
                             (sel_sb, sel_d), (tril_sb, tril_d)):
                Sx.dma("sp", "const", lambda e, d=dst, s=src: e.dma_start(out=d[:], in_=s), accw=[bConst])
            Sx.op("pool", lambda e: e.memset(ones_bf[:], 1.0), accw=[bConst])
            Sx.op("pool", lambda e: e.memset(ones_row[:], 1.0), accw=[bConst])
            Sx.op("pool", lambda e: e.memset(eps_sb[:], EPS), accw=[bConst])
            Sx.dma("sp", "const", lambda e: e.dma_start(out=perm_f[:], in_=perm_d), accw=[bConst])
            Sx.op("dve", lambda e: e.tensor_copy(out=ident_bf[:], in_=ident_f[:]), reads=[bConst], accw=[bConst])
            Sx.op("dve", lambda e: e.tensor_copy(out=perm_bf[:], in_=perm_f[:]), reads=[bConst], accw=[bConst])

        def bank(i):
            return ps[:, i, :]

        def norm_chain_a(hbuf, bHb, sq, bSq, pst_i):
            Sx.op("pool", lambda e: e.tensor_tensor(out=sq[:].rearrange("p k t -> p (k t)"),
                                                    in0=hbuf[:].rearrange("p k t -> p (k t)"),
                                                    in1=hbuf[:].rearrange("p k t -> p (k t)"), op=ALU.mult),
                  reads=[bHb], writes=[bSq])
            for kc in range(KC):
                Sx.op("pe", lambda e, kc=kc: e.matmul(bank(pst_i), lhsT=ones_bf[:], rhs=sq[:, kc, :],
                                                      start=(kc == 0), stop=(kc == KC - 1)),
                      reads=[bSq, bConst], writes=[psB[pst_i]] if kc == 0 else (), accw=[psB[pst_i]] if kc else (),
                      sig=(kc == KC - 1))

        def norm_chain_rstd(pst_i, lnt, bLn, rstd, bRs):
            Sx.op("act", lambda e: e.activation(out=lnt[:], in_=bank(pst_i), func=AF.Ln, bias=eps_sb[:],
                                                scale=1.0 / D),
                  reads=[psB[pst_i], bConst], writes=[bLn])
            Sx.op("act", lambda e: e.activation(out=rstd[:], in_=lnt[:], func=AF.Exp, scale=-0.5),
                  reads=[bLn], writes=[bRs])

        def modulate(hbuf, bHb, rstd, bRs, tmps, bTmps, yT, bY, l, i):
            base = (l * 3 + i) * KC
            shbase = l * 72 + (3 * i) * KC
            for kc in range(KC):
                s = kc % len(tmps)
                Sx.op("dve", lambda e, kc=kc, s=s: e.scalar_tensor_tensor(
                    out=tmps[s][:], in0=hbuf[:, kc, :], scalar=Av[:, base + kc:base + kc + 1], in1=rstd[:],
                    op0=ALU.mult, op1=ALU.mult), reads=[bHb, bRs, bMod], writes=[bTmps[s]])
                Sx.op("act", lambda e, kc=kc, s=s: e.activation(
                    out=yT[:, kc, :], in_=tmps[s][:], func=AF.Identity,
                    bias=modv[:, shbase + kc:shbase + kc + 1], scale=1.0), reads=[bTmps[s], bMod], writes=[bY[kc]])

        def load_weight(dst, src2d, nchunk, rows, key, bW):
            first = True
            for k in range(nchunk):
                Sx.dma("pool", key, lambda e, k=k: e.dma_start(out=dst[:, k, :], in_=src2d[k * 128:(k + 1) * 128, :]),
                       writes=[bW] if first else (), accw=() if first else [bW])
                first = False

        with arena_scope() as ph:
            wa = [sb("wa%d" % i, [128, KC, 512], BF16, ph) for i in range(3)]
            bWa = [NB() for _ in range(3)]
            Sx.op("act", lambda e: e.activation(out=cact[:], in_=c_sb[:], func=AF.Silu), reads=[bConst], accw=[bConst])
            gi = 0
            for l in range(DEPTH):
                src = ada_w[l].rearrange("(k p) f -> p k f", p=128)
                for g in range(18):
                    s = gi % 3
                    gi += 1
                    Sx.dma("pool", "wa%d" % s, lambda e, s=s, g=g, src=src: e.dma_start(
                        out=wa[s][:], in_=src[:, :, g * 512:(g + 1) * 512]), writes=[bWa[s]])
                    for jj in range(4):
                        col = l * 72 + g * 4 + jj
                        for kc in range(KC):
                            Sx.op("pe", lambda e, s=s, jj=jj, kc=kc, col=col: e.matmul(
                                ps[:, 0, col:col + 1], lhsT=wa[s][:, kc, jj * 128:(jj + 1) * 128],
                                rhs=cact[:, kc:kc + 1], start=(kc == 0), stop=(kc == KC - 1)),
                                reads=[bWa[s], bConst], accw=[psB[0]], sig=(kc == KC - 1))
            Sx.op("dve", lambda e: e.tensor_tensor(out=modv[:], in0=ps[:, 0, 0:DEPTH * 72], in1=adab_sb[:], op=ALU.add),
                  reads=[psB[0], bConst], writes=[bMod])
            for l in range(DEPTH):
                for i in range(3):
                    b0 = (l * 3 + i) * KC
                    sc0 = l * 72 + (3 * i + 1) * KC
                    g0 = l * 72 + (3 * i + 2) * KC
                    Sx.op("dve", lambda e, b0=b0, sc0=sc0: e.scalar_tensor_tensor(
                        out=Av[:, b0:b0 + KC], in0=modv[:, sc0:sc0 + KC], scalar=1.0, in1=ng_sb[:, b0:b0 + KC],
                        op0=ALU.add, op1=ALU.mult), reads=[bMod, bConst], accw=[bMod])
                    Sx.op("dve", lambda e, b0=b0, g0=g0, i=i: e.tensor_scalar(
                        out=Gv[:, b0:b0 + KC], in0=modv[:, g0:g0 + KC], scalar1=(1.0 if i == 1 else 0.5), scalar2=None,
                        op0=ALU.mult), reads=[bMod], accw=[bMod])
            if debug:
                out_toks.append(Sx.dma("sp", "dbg", lambda e: e.dma_start(out=dbg_mod, in_=modv[:]), reads=[bMod]))

        def hsrc_tile(src, t):
            return src.rearrange("(k p) s -> p k s", p=128)[:, :, t * T:(t + 1) * T]

        def residual_phase_tail(m, t, src, dst, hres, bHres, ps_i, gcol, is_last_reader_buf=None):
            pass

        def ffn_phase(l, which, src, pre=None):
            i = 0 if which == 0 else 2
            with arena_scope() as ph:
                Wg = sb("Wg", [128, KC, DFF], BF16, ph)
                Wu = sb("Wu", [128, KC, DFF], BF16, ph)
                Wd = sb("Wd", [128, FC, D], BF16, ph)
                bWg, bWu, bWd = NB(), NB(), NB()
                if pre is not None:
                    pre()
                CBS = [(0, 6), (6, 12), (12, 17), (17, 22)]
                bWgc = [NB() for _ in CBS]
                bWuc = [NB() for _ in CBS]
                cb_of = {}
                for ci_, (j0, j1) in enumerate(CBS):
                    for j_ in range(j0, j1):
                        cb_of[j_] = ci_
                    for (Wx, wsrc, bWx, kname) in ((Wg, wg_d[which][l], bWgc, "wg%d"), (Wu, wu_d[which][l], bWuc, "wu%d")):
                        for k in range(KC):
                            Sx.dma("pool", kname % ci_, lambda e, k=k, j0=j0, j1=j1, Wx=Wx, wsrc=wsrc: e.dma_start(
                                out=Wx[:, k, j0 * 128:j1 * 128], in_=wsrc[k * 128:(k + 1) * 128, j0 * 128:j1 * 128]),
                                writes=[bWx[ci_]] if k == 0 else (), accw=[bWx[ci_]] if k else ())
                load_weight(Wd, wd_d[which][l], FC, DFF, "wd", bWd)
                hbuf = sb("hbuf", [128, KC, T], F32, ph)
                sq = sb("sq", [128, KC, T], BF16, ph)
                yT = sb("yT", [128, KC, T], BF16, ph)
                actT = sb("actT", [128, FC, T], BF16, ph)
                tmps = [sb("tmp%d" % j, [128, T], F32, ph) for j in range(2)]
                sgs = [sb("sg%d" % j, [128, T], F32, ph) for j in range(2)]
                rstd = sb("rstd", [128, T], F32, ph)
                hres = [sb("hres%d" % j, [128, T], F32, ph) for j in range(2)]
                bHb, bSq, bLn, bRs = NB(), NB(), NB(), NB()
                bY = [NB() for _ in range(KC)]
                bAct = [NB() for _ in range(FC)]
                bTmps = [NB() for _ in range(2)]
                bSg = [NB() for _ in range(2)]
                bHres = [NB() for _ in range(2)]
                PST, PG, PU, PD = 0, (1, 2), (3, 4), (5, 6)

                def load_h(t):
                    Sx.dma("sp", "h", lambda e, t=t: e.dma_start(out=hbuf[:], in_=hsrc_tile(src, t)),
                           reads=[bH[t]] if src is hS else (), writes=[bHb])

                def chain(t):
                    norm_chain_a(hbuf, bHb, sq, bSq, PST)

                def chain_b(t):
                    norm_chain_rstd(PST, rstd, bRs, rstd, bRs)
                    modulate(hbuf, bHb, rstd, bRs, tmps, bTmps, yT, bY, l, i)

                load_h(0)
                chain(0)
                chain_b(0)
                srcr = src.rearrange("(k p) s -> k p s", p=128)
                dstr = hS.rearrange("(k p) s -> k p s", p=128)
                gbase = (l * 3 + i) * KC
                hrc = 0
                for t in range(NTR):
                    if t + 1 < NTR:
                        load_h(t + 1)
                    for j in range(FC):
                        pg, pu = PG[j % 2], PU[j % 2]
                        for kc in range(KC):
                            Sx.op("pe", lambda e, j=j, kc=kc, pg=pg: e.matmul(
                                bank(pg), lhsT=Wg[:, kc, j * 128:(j + 1) * 128], rhs=yT[:, kc, :],
                                start=(kc == 0), stop=(kc == KC - 1)),
                                reads=[bWgc[cb_of[j]], bY[kc]], writes=[psB[pg]] if kc == 0 else (),
                                accw=[psB[pg]] if kc else (), sig=(kc == KC - 1))
                        for kc in range(KC):
                            Sx.op("pe", lambda e, j=j, kc=kc, pu=pu: e.matmul(
                                bank(pu), lhsT=Wu[:, kc, j * 128:(j + 1) * 128], rhs=yT[:, kc, :],
                                start=(kc == 0), stop=(kc == KC - 1)),
                                reads=[bWuc[cb_of[j]], bY[kc]], writes=[psB[pu]] if kc == 0 else (),
                                accw=[psB[pu]] if kc else (), sig=(kc == KC - 1))
                        sgi = j % 2
                        Sx.op("act", lambda e, pg=pg, sgi=sgi: e.activation(out=sgs[sgi][:], in_=bank(pg), func=AF.Silu),
                              reads=[psB[pg]], writes=[bSg[sgi]])
                        Sx.op("dve", lambda e, j=j, pu=pu, sgi=sgi: e.tensor_tensor(
                            out=actT[:, j, :], in0=bank(pu), in1=sgs[sgi][:], op=ALU.mult),
                            reads=[psB[pu], bSg[sgi]], writes=[bAct[j]])
                    if t + 1 < NTR:
                        chain(t + 1)
                    def load_hres(m, t=t):
                        hs = m % 2
                        Sx.dma("sp", "hres%d" % hs, lambda e, m=m, t=t, hs=hs: e.dma_start(
                            out=hres[hs][:], in_=srcr[m, :, t * T:(t + 1) * T]),
                            reads=[bH[t]] if src is hS else (), writes=[bHres[hs]])

                    load_hres(0)
                    for m in range(KC):
                        pd = PD[m % 2]
                        hs = m % 2
                        if m + 1 < KC:
                            load_hres(m + 1)
                        for j in range(FC):
                            Sx.op("pe", lambda e, j=j, m=m, pd=pd: e.matmul(
                                bank(pd), lhsT=Wd[:, j, m * 128:(m + 1) * 128], rhs=actT[:, j, :],
                                start=(j == 0), stop=(j == FC - 1)),
                                reads=[bWd, bAct[j]], writes=[psB[pd]] if j == 0 else (),
                                accw=[psB[pd]] if j else (), sig=(j == FC - 1))
                        Sx.op("dve", lambda e, m=m, pd=pd, hs=hs: e.scalar_tensor_tensor(
                            out=hres[hs][:], in0=bank(pd), scalar=Gv[:, gbase + m:gbase + m + 1], in1=hres[hs][:],
                            op0=ALU.mult, op1=ALU.add), reads=[psB[pd], bMod], writes=[bHres[hs]])
                        Sx.dma("sp", "hres%d" % hs, lambda e, m=m, t=t, hs=hs: e.dma_start(
                            out=dstr[m, :, t * T:(t + 1) * T], in_=hres[hs][:]),
                            reads=[bHres[hs]], writes=[bH[t]] if m == 0 else (), accw=[bH[t]] if m else ())
                        if m == 1 and t + 1 < NTR:
                            chain_b(t + 1)

        def proj_phase(l):
            i = 1
            with arena_scope() as ph:
                Win = sb("Win", [128, KC, INW], BF16, ph)
                wsT_f = sb("wsT_f", [128, 4, 128], F32, ph)
                wsT_bf = sb("wsT_bf", [128, 4, 128], BF16, ph)
                lng = sb("lng", [128, 512], F32, ph)
                lnb = sb("lnb", [128, 512], F32, ph)
                bsr = sb("bsr", [1, 512], F32, ph)
                hbuf = sb("hbuf", [128, KC, T], F32, ph)
                sq = sb("sq", [128, KC, T], BF16, ph)
                yTs = [sb("yT%d" % j, [128, KC, T], BF16, ph) for j in range(2)]
                tmps = [sb("tmp%d" % j, [128, T], F32, ph) for j in range(2)]
                lnt = sb("lnt", [128, T], F32, ph)
                rstd = sb("rstd", [128, T], F32, ph)
                ug = sb("ug", [128, 4, T], F32, ph)
                vg4 = sb("vg4", [128, 4, 512], F32, ph)
                vsq = [sb("vsq%d" % j, [128, 512], F32, ph) for j in range(2)]
                s1 = sb("s1", [128, 16], F32, ph)
                s2 = sb("s2", [128, 16], F32, ph)
                mean = sb("mean", [128, 16], F32, ph)
                msq = sb("msq", [128, 16], F32, ph)
                var = sb("var", [128, 16], F32, ph)
                lnv = sb("lnv", [128, 16], F32, ph)
                rs = sb("rs", [128, 16], F32, ph)
                vnf = [sb("vnf%d" % j, [128, 512], F32, ph) for j in range(2)]
                vnt = [sb("vnt%d" % j, [128, 512], F32, ph) for j in range(2)]
                vnb = [sb("vnb%d" % j, [128, 512], BF16, ph) for j in range(4)]
                cs = [sb("cs%d" % j, [128, 2, T], F32, ph) for j in range(2)]
                t1 = [sb("t1_%d" % j, [128, T], F32, ph) for j in range(2)]
                t2 = [sb("t2_%d" % j, [128, T], F32, ph) for j in range(2)]
                oa = [sb("oa%d" % j, [128, 4, T], BF16, ph) for j in range(2)]
                qr = [sb("qr%d" % j, [128, 4, T], BF16, ph) for j in range(2)]
                kr = [sb("kr%d" % j, [128, 4, T], BF16, ph) for j in range(2)]
                va = [sb("va%d" % j, [128, 4, 8, 65], BF16, ph) for j in range(2)]
                bWin, bWs, bLnc, bHb, bSq, bLn, bRs, bUg = (NB() for _ in range(8))
                bYs = [[NB() for _ in range(KC)] for _ in range(2)]
                bTmps = [NB() for _ in range(2)]
                bVg = [NB() for _ in range(4)]
                bVsq = [NB() for _ in range(2)]
                bStat = NB()
                bVnf = [NB() for _ in range(2)]
                bVnt = [NB() for _ in range(2)]
                bVnb = [NB() for _ in range(4)]
                bCs = [NB() for _ in range(2)]
                bT1 = [NB() for _ in range(2)]
                bT2 = [NB() for _ in range(2)]
                bOa = [NB() for _ in range(2)]
                bQr = [NB() for _ in range(2)]
                bKr = [NB() for _ in range(2)]
                bVa = [NB() for _ in range(2)]
                load_weight(Win, win_d[l], KC, D, "wg", bWin)
                Sx.dma("sp", "const", lambda e: e.dma_start(out=wsT_f[:], in_=wsT_d[l]), writes=[bLnc])
                Sx.dma("sp", "const", lambda e: e.dma_start(out=lng[:], in_=lng_d[l]), accw=[bLnc])
                Sx.dma("sp", "const", lambda e: e.dma_start(out=lnb[:], in_=lnb_d[l]), accw=[bLnc])
                Sx.dma("sp", "const", lambda e: e.dma_start(out=bsr[:], in_=bs_d[l]), accw=[bLnc])
                tril_b = cust(tril_sb, 0, [[0, 4], [1, 128]])
                Sx.op("dve", lambda e: e.tensor_tensor(out=wsT_bf[:], in0=wsT_f[:], in1=tril_b, op=ALU.mult),
                      reads=[bLnc, bConst], writes=[bWs])
                for j in range(2):
                    Sx.op("pool", lambda e, j=j: e.memset(va[j][:], 1.0), writes=[bVa[j]])
                PST, PZ = 0, 1
                qsb = [sb("qsb%d" % j, [128, T], BF16, ph) for j in range(2)]
                bQsb = [NB() for _ in range(2)]
                qctr = [0]
                work = [2, 3, 4, 5, 6, 7]
                wk = [0]

                def nb_():
                    b = work[wk[0] % 6]
                    wk[0] += 1
                    return b

                def load_h(t):
                    Sx.dma("sp", "h", lambda e, t=t: e.dma_start(out=hbuf[:], in_=hsrc_tile(hS, t)),
                           reads=[bH[t]], writes=[bHb])

                def mm8(pb, lhs_fn, rhs_fn, rd):
                    for kc in range(KC):
                        lhs = lhs_fn(kc)
                        rhs = rhs_fn(kc)
                        Sx.op("pe", lambda e, kc=kc, lhs=lhs, rhs=rhs: e.matmul(
                            bank(pb), lhsT=lhs, rhs=rhs, start=(kc == 0), stop=(kc == KC - 1)),
                            reads=[bWin, cur["bY"][kc]] + rd, writes=[psB[pb]] if kc == 0 else (),
                            accw=[psB[pb]] if kc else (), sig=(kc == KC - 1))

                load_h(0)
                norm_chain_a(hbuf, bHb, sq, bSq, PST)
                norm_chain_rstd(PST, lnt, bLn, rstd, bRs)
                modulate(hbuf, bHb, rstd, bRs, tmps, bTmps, yTs[0], bYs[0], l, i)
                cur = {}
                for t in range(NT):
                    sl = t % 2
                    cur["yT"] = yTs[t % 2]
                    cur["bY"] = bYs[t % 2]
                    yT = yTs[t % 2]
                    Sx.dma("sp", "cs%d" % sl, lambda e, t=t, sl=sl: e.dma_start(
                        out=cs[sl][:], in_=rope_d.rearrange("c p s -> p c s")[:, :, t * T:(t + 1) * T]), writes=[bCs[sl]])
                    if t + 1 < NT:
                        load_h(t + 1)
                    for hd in range(4):
                        pb = nb_()
                        mm8(pb, lambda kc, hd=hd: Win[:, kc, hd * 128:(hd + 1) * 128], lambda kc: yT[:, kc, :], [])
                        Sx.op("act", lambda e, pb=pb, hd=hd: e.activation(out=ug[:, hd, :], in_=bank(pb),
                                                                          func=AF.Gelu_apprx_tanh),
                              reads=[psB[pb]], writes=[bUg] if hd == 0 else (), accw=[bUg] if hd else ())
                    for s in range(4):
                        pb = nb_()
                        mm8(pb, lambda kc, s=s: yT[:, kc, s * 128:(s + 1) * 128], lambda kc: Win[:, kc, 512:1024], [])
                        Sx.op("act", lambda e, pb=pb, s=s: e.activation(out=vg4[:, s, :], in_=bank(pb),
                                                                        func=AF.Gelu_apprx_tanh),
                              reads=[psB[pb]], writes=[bVg[s]])
                        Sx.op("act", lambda e, s=s: e.activation(out=vsq[s % 2][:], in_=vg4[:, s, :], func=AF.Square),
                              reads=[bVg[s]], writes=[bVsq[s % 2]])
                        Sx.op("dve", lambda e, s=s: e.tensor_reduce(
                            out=s1[:, s * 4:(s + 1) * 4], in_=vg4[:, s, :].rearrange("p (h c) -> p h c", h=4),
                            axis=AX.X, op=ALU.add), reads=[bVg[s]], writes=[bStat] if s == 0 else (),
                            accw=[bStat] if s else ())
                        Sx.op("dve", lambda e, s=s: e.tensor_reduce(
                            out=s2[:, s * 4:(s + 1) * 4], in_=vsq[s % 2][:].rearrange("p (h c) -> p h c", h=4),
                            axis=AX.X, op=ALU.add), reads=[bVsq[s % 2]], accw=[bStat])
                    Sx.op("dve", lambda e: e.tensor_scalar(out=mean[:], in0=s1[:], scalar1=1.0 / 128, scalar2=None,
                                                           op0=ALU.mult), reads=[bStat], accw=[bStat])
                    Sx.op("dve", lambda e: e.tensor_tensor(out=msq[:], in0=mean[:], in1=mean[:], op=ALU.mult),
                          reads=[bStat], accw=[bStat])
                    Sx.op("dve", lambda e: e.scalar_tensor_tensor(out=var[:], in0=s2[:], scalar=1.0 / 128, in1=msq[:],
                                                                  op0=ALU.mult, op1=ALU.subtract),
                          reads=[bStat], accw=[bStat])
                    Sx.op("act", lambda e: e.activation(out=lnv[:], in_=var[:], func=AF.Ln, bias=eps_sb[:], scale=1.0),
                          reads=[bStat, bConst], accw=[bStat])
                    Sx.op("act", lambda e: e.activation(out=rs[:], in_=lnv[:], func=AF.Exp, scale=-0.5),
                          reads=[bStat], accw=[bStat])
                    if t + 1 < NT:
                        norm_chain_a(hbuf, bHb, sq, bSq, PST)
                    for s in range(4):
                        x = s % 2
                        for hd in range(4):
                            idx = s * 4 + hd
                            Sx.op("dve", lambda e, s=s, hd=hd, idx=idx, x=x: e.tensor_scalar(
                                out=vnf[x][:, hd * 128:(hd + 1) * 128], in0=vg4[:, s, hd * 128:(hd + 1) * 128],
                                scalar1=mean[:, idx:idx + 1], scalar2=rs[:, idx:idx + 1],
                                op0=ALU.subtract, op1=ALU.mult),
                                reads=[bVg[s], bStat], writes=[bVnf[x]] if hd == 0 else (), accw=[bVnf[x]] if hd else ())
                        Sx.op("pool", lambda e, x=x: e.tensor_tensor(out=vnt[x][:], in0=vnf[x][:], in1=lng[:], op=ALU.mult),
                              reads=[bVnf[x], bLnc], writes=[bVnt[x]])
                        Sx.op("pool", lambda e, x=x, s=s: e.tensor_tensor(out=vnb[s][:], in0=vnt[x][:], in1=lnb[:], op=ALU.add),
                              reads=[bVnt[x], bLnc], writes=[bVnb[s]])
                    for (dstb, bDst, c0, scl) in ((qr, bQr, 1024, 0.125), (kr, bKr, 1536, 1.0)):
                        for c in range(4):
                            pa, pp = nb_(), nb_()
                            mm8(pa, lambda kc, c=c, c0=c0: Win[:, kc, c0 + c * 128:c0 + (c + 1) * 128],
                                lambda kc: yT[:, kc, :], [])
                            qx = qctr[0] % 2
                            qctr[0] += 1
                            tok_cp = Sx.op("act", lambda e, pa=pa, qx=qx: e.activation(out=qsb[qx][:], in_=bank(pa), func=AF.Identity),
                                           reads=[psB[pa]], writes=[bQsb[qx]])
                            Sx.op("pe", lambda e, pp=pp, qx=qx: e.matmul(bank(pp), lhsT=perm_bf[:], rhs=qsb[qx][:],
                                                                        start=True, stop=True),
                                  reads=[bQsb[qx], bConst], writes=[psB[pp]])
                            x = c % 2
                            Sx.op("dve", lambda e, pa=pa, x=x, scl=scl, sl=sl: e.scalar_tensor_tensor(
                                out=t1[x][:], in0=bank(pa), scalar=scl, in1=cs[sl][:, 0, :], op0=ALU.mult, op1=ALU.mult),
                                reads=[psB[pa], bCs[sl]], writes=[bT1[x]], extra=[tok_cp])
                            Sx.op("dve", lambda e, pp=pp, x=x, scl=scl, sl=sl: e.scalar_tensor_tensor(
                                out=t2[x][:], in0=bank(pp), scalar=scl, in1=cs[sl][:, 1, :], op0=ALU.mult, op1=ALU.mult),
                                reads=[psB[pp], bCs[sl]], writes=[bT2[x]])
                            Sx.op("pool", lambda e, x=x, c=c, dstb=dstb, sl=sl: e.tensor_tensor(
                                out=dstb[sl][:, c, :], in0=t1[x][:], in1=t2[x][:], op=ALU.add),
                                reads=[bT1[x], bT2[x]], writes=[bDst[sl]] if c == 0 else (), accw=[bDst[sl]] if c else ())
                    Sx.dma("sp", "qr%d" % sl, lambda e, t=t, sl=sl: e.dma_start(
                        out=qS.rearrange("(c p) s -> p c s", p=128)[:, :, t * T:(t + 1) * T], in_=qr[sl][:]),
                        reads=[bQr[sl]], writes=[bQd] if t == 0 else (), accw=[bQd] if t else ())
                    Sx.dma("sp", "kr%d" % sl, lambda e, t=t, sl=sl: e.dma_start(
                        out=kS.rearrange("(c p) s -> p c s", p=128)[:, :, t * T:(t + 1) * T], in_=kr[sl][:]),
                        reads=[bKr[sl]], writes=[bKd] if t == 0 else (), accw=[bKd] if t else ())
                    if t + 1 < NT:
                        norm_chain_rstd(PST, lnt, bLn, rstd, bRs)
                        modulate(hbuf, bHb, rstd, bRs, tmps, bTmps, yTs[(t + 1) % 2], bYs[(t + 1) % 2], l, i)

                    for s in range(4):
                        pb = nb_()
                        mm8(pb, lambda kc, s=s: yT[:, kc, s * 128:(s + 1) * 128], lambda kc: Win[:, kc, 2048:2560], [])
                        Sx.op("act", lambda e, pb=pb, s=s, sl=sl: e.activation(
                            out=va[sl][:, s, :, 0:64], in_=bank(pb).rearrange("p (h c) -> p h c", h=8), func=AF.Identity),
                            reads=[psB[pb]], accw=[bVa[sl]], extra=list(((k[0], k[1], n) for k, n in bVa[sl].r.items())))
                    Sx.dma("sp", "va%d" % sl, lambda e, t=t, sl=sl: e.dma_start(
                        out=vS[t * T:(t + 1) * T, :].rearrange("(s p) c -> p s c", p=128),
                        in_=va[sl][:].rearrange("p s h c -> p s (h c)")),
                        reads=[bVa[sl]], writes=[bVd] if t == 0 else (), accw=[bVd] if t else ())
                    for s in range(4):
                        for hd in range(4):
                            Sx.op("pe", lambda e, s=s, hd=hd: e.matmul(
                                ps[:, PZ, hd * 128:(hd + 1) * 128], lhsT=vnb[s][:, hd * 128:(hd + 1) * 128],
                                rhs=wsT_bf[:, hd, :], start=True, stop=False),
                                reads=[bVnb[s], bWs], writes=[psB[PZ]] if hd == 0 else (), accw=[psB[PZ]] if hd else (),
                                sig=False)
                            Sx.op("pe", lambda e, s=s, hd=hd: e.matmul(
                                ps[:, PZ, hd * 128:(hd + 1) * 128], lhsT=ones_row[0:1, :],
                                rhs=bsr[0:1, hd * 128:(hd + 1) * 128], start=False, stop=True),
                                reads=[bLnc, bConst], accw=[psB[PZ]], sig=(hd == 3))
                        Sx.op("dve", lambda e, s=s, sl=sl: e.tensor_tensor(
                            out=oa[sl][:, :, s * 128:(s + 1) * 128],
                            in0=ps[:, PZ, :].rearrange("p (h c) -> p h c", h=4),
                            in1=ug[:, :, s * 128:(s + 1) * 128], op=ALU.mult),
                            reads=[psB[PZ], bUg], writes=[bOa[sl]] if s == 0 else (), accw=[bOa[sl]] if s else ())
                    Sx.dma("sp", "oa%d" % sl, lambda e, t=t, sl=sl: e.dma_start(
                        out=mS[0:512, :].rearrange("(c p) s -> p c s", p=128)[:, :, t * T:(t + 1) * T], in_=oa[sl][:]),
                        reads=[bOa[sl]], writes=[bMd[t]])
        def attn_phase(l):
            with arena_scope() as ph:
                Qs = [sb("Qs%d" % j, [128, S], BF16, ph) for j in range(2)]
                Ks = [sb("Ks%d" % j, [128, S], BF16, ph) for j in range(2)]
                VR = {r: sb("VR%d" % r, [128, 32, 520], BF16, ph) for r in (1, 4, 16)}
                acc = [sb("acc%d" % j, [128, S], F32, ph) for j in range(2)]
                pT = [sb("pT%d" % j, [128, 2, 4, 128], BF16, ph) for j in range(3)]
                ob = [sb("ob%d" % j, [128, S], BF16, ph) for j in range(2)]
                sm = [sb("sm%d" % j, [128, 2, 512], F32, ph) for j in range(3)]
                rb = [sb("rb%d" % j, [64, 512], F32, ph) for j in range(1)] * 2
                bSm = [[NB(), NB()] for _ in range(3)]
                bRb = [NB()] * 2
                bQ = [NB() for _ in range(2)]
                bK = [NB() for _ in range(2)]
                bVR = {r: NB() for r in (1, 4, 16)}
                bAcc = [NB() for _ in range(2)]
                bPT = [[NB(), NB()] for _ in range(3)]
                bOb = [NB() for _ in range(2)]
                mb_bf = sb("mb_bf", [128, 4, 512], BF16, ph)
                bMb = NB()
                mb_f = cust(sm[0], 0, [[512, 4], [1, 512]])
                Sx.dma("sp", "const", lambda e: e.dma_start(out=mb_f, in_=mb_d.rearrange("v p f -> p v f")), writes=[bSm[0][0]],
                       accw=[bSm[0][1], bSm[1][0], bSm[1][1]])
                Sx.op("dve", lambda e: e.tensor_copy(out=mb_bf[:], in_=mb_f), reads=[bSm[0][0], bSm[0][1], bSm[1][0], bSm[1][1]], writes=[bMb])
                for r in (1, 4, 16):
                    nbk = 32 // r
                    for rho in range(r):
                        srcv = bass.AP(vS.tensor, rho * 520, [[r * 520, 128], [128 * r * 520, nbk], [1, 520]])
                        Sx.dma("sp", "vr%d" % r, lambda e, r=r, rho=rho, nbk=nbk, srcv=srcv: e.dma_start(
                            out=VR[r][:, rho * nbk:(rho + 1) * nbk, :], in_=srcv),
                            reads=[bVd], writes=[bVR[r]] if rho == 0 else (), accw=[bVR[r]] if rho else ())
                PSS = ((0, 1), (2, 3), (4, 5))
                PSO = (6, 7)
                PSN = (6, 7)
                octr = [0]
                nctr = [0]

                def load_qk(head):
                    sl = head % 2
                    for (dst, srcS, bD, bS_) in ((Qs, qS, bQ, bQd), (Ks, kS, bK, bKd)):
                        for dup in range(2):
                            Sx.dma("sp", "qk%d" % sl, lambda e, head=head, sl=sl, dup=dup, dst=dst, srcS=srcS: e.dma_start(
                                out=dst[sl][dup * 64:(dup + 1) * 64, :], in_=srcS[head * 64:(head + 1) * 64, :]),
                                reads=[bS_], writes=[bD[sl]] if dup == 0 else (), accw=[bD[sl]] if dup else ())

                groups = []
                for head in range(8):
                    for r in (1, 4, 16):
                        for g in range(8):
                            groups.append((head, r, g))

                def blocks_of(r, g):
                    nbk = 32 // r
                    out = []
                    for b in range(4):
                        bi = 4 * g + b
                        rho, n = divmod(bi, nbk)
                        out.append((bi, rho, n))
                    return out

                def emit_qk(i):
                    head, r, g = groups[i]
                    sl = head % 2
                    pss = PSS[i % 3]
                    px = i % 3
                    blocks = blocks_of(r, g)
                    pP, pC = pss
                    for b in range(4):
                        bi, rho, n = blocks[b]
                        q0 = rho + r * 128 * n
                        p0 = rho + r * 128 * (n - 1) if n > 0 else q0
                        qsl = slice(q0, q0 + r * 127 + 1, r)
                        psl = slice(p0, p0 + r * 127 + 1, r)
                        Sx.op("pe", lambda e, pP=pP, b=b, psl=psl, qsl=qsl, sl=sl: e.matmul(
                            ps[:, pP, b * 128:(b + 1) * 128], lhsT=Ks[sl][0:64, psl], rhs=Qs[sl][0:64, qsl],
                            start=True, stop=True),
                            reads=[bQ[sl], bK[sl]], writes=[psB[pP]] if b == 0 else (), accw=[psB[pP]] if b else (),
                            sig=False)
                        Sx.op("pe", lambda e, pC=pC, b=b, qsl=qsl, sl=sl: e.matmul(
                            ps[:, pC, b * 128:(b + 1) * 128], lhsT=Ks[sl][64:128, qsl], rhs=Qs[sl][64:128, qsl],
                            start=True, stop=True),
                            reads=[bQ[sl], bK[sl]], writes=[psB[pC]] if b == 0 else (), accw=[psB[pC]] if b else (),
                            sig=(b == 3))
                    if r == 16:
                        vP = 2
                    else:
                        vP = 0 if blocks[0][2] == 0 else 1
                    for half, (pbk, variant) in enumerate(((pP, vP), (pC, 3))):
                        Sx.op("dve", lambda e, pbk=pbk, px=px, half=half, variant=variant: e.tensor_tensor(
                            out=sm[px][:, half, :], in0=bank(pbk), in1=mb_bf[:, variant, :], op=ALU.add),
                            reads=[psB[pbk], bMb], writes=[bSm[px][half]])
                        Sx.op("act", lambda e, px=px, half=half: e.activation(
                            out=pT[px][:, half, :, :].rearrange("p b c -> p (b c)"),
                            in_=sm[px][:, half, :], func=AF.Exp), reads=[bSm[px][half]], writes=[bPT[px][half]])

                def emit_pv(i):
                    head, r, g = groups[i]
                    a = head % 2
                    pso = PSO[octr[0] % 2]
                    octr[0] += 1
                    px = i % 3
                    blocks = blocks_of(r, g)
                    for b in range(4):
                        bi, rho, n = blocks[b]
                        first = (b == 0)
                        if n > 0:
                            Sx.op("pe", lambda e, pso=pso, b=b, bi=bi, head=head, px=px, r=r: e.matmul(
                                ps[0:65, pso, b * 128:(b + 1) * 128],
                                lhsT=VR[r][:, bi - 1, head * 65:(head + 1) * 65], rhs=pT[px][:, 0, b, :],
                                start=True, stop=False),
                                reads=[bVR[r], bPT[px][0]], writes=[psB[pso]] if first else (),
                                accw=() if first else [psB[pso]], sig=False)
                            first = False
                        Sx.op("pe", lambda e, pso=pso, b=b, bi=bi, head=head, px=px, r=r, n=n: e.matmul(
                            ps[0:65, pso, b * 128:(b + 1) * 128],
                            lhsT=VR[r][:, bi, head * 65:(head + 1) * 65], rhs=pT[px][:, 1, b, :],
                            start=(n == 0), stop=True),
                            reads=[bVR[r], bPT[px][1]], writes=[psB[pso]] if first else (),
                            accw=() if first else [psB[pso]], sig=(b == 3))
                    if r == 1:
                        Sx.op("act", lambda e, pso=pso, g=g, a=a: e.activation(
                            out=acc[a][0:65, g * 512:(g + 1) * 512], in_=ps[0:65, pso, :], func=AF.Identity),
                            reads=[psB[pso]], writes=[bAcc[a]] if g == 0 else (), accw=[bAcc[a]] if g else ())
                    elif r == 4:
                        rho = g // 2
                        o0 = rho + 2048 * (g % 2)
                        Sx.op("dve", lambda e, pso=pso, o0=o0, a=a: e.tensor_tensor(
                            out=acc[a][0:65, o0:o0 + 2045:4], in0=ps[0:65, pso, :],
                            in1=acc[a][0:65, o0:o0 + 2045:4], op=ALU.add),
                            reads=[psB[pso], bAcc[a]], accw=[bAcc[a]])
                    else:
                        av = cust(acc[a], 2 * g, [[1, 2], [16, 256]], nparts=65)
                        Sx.op("dve", lambda e, pso=pso, av=av: e.tensor_tensor(
                            out=av, in0=ps[0:65, pso, :].rearrange("p (a j) -> p a j", a=2),
                            in1=av, op=ALU.add),
                            reads=[psB[pso], bAcc[a]], accw=[bAcc[a]])

                def emit_norm(head):
                    a = head % 2
                    for sli in range(8):
                        pn = PSO[octr[0] % 2]
                        octr[0] += 1
                        rx = nctr[0] % 2
                        nctr[0] += 1
                        Sx.op("pe", lambda e, pn=pn, sli=sli, a=a: e.matmul(
                            ps[0:64, pn, :], lhsT=sel_sb[0:65, :], rhs=acc[a][0:65, sli * 512:(sli + 1) * 512],
                            start=True, stop=True), reads=[bAcc[a], bConst], writes=[psB[pn]])
                        Sx.op("act", lambda e, pn=pn, rx=rx: e.activation(out=rb[rx][:], in_=ps[0:64, pn, :], func=AF.Ln),
                              reads=[psB[pn]], writes=[bRb[rx]])
                        Sx.op("act", lambda e, rx=rx: e.activation(out=rb[rx][:], in_=rb[rx][:], func=AF.Exp, scale=-1.0),
                              reads=[bRb[rx]], writes=[bRb[rx]])
                        Sx.op("pool", lambda e, sli=sli, a=a, rx=rx: e.tensor_tensor(
                            out=ob[a][0:64, sli * 512:(sli + 1) * 512], in0=rb[rx][:],
                            in1=acc[a][0:64, sli * 512:(sli + 1) * 512], op=ALU.mult),
                            reads=[bRb[rx], bAcc[a]], writes=[bOb[a]] if sli == 0 else (), accw=[bOb[a]] if sli else ())
                    Sx.dma("sp", "ob%d" % a, lambda e, head=head, a=a: e.dma_start(
                        out=mS[512 + head * 64:512 + (head + 1) * 64, :], in_=ob[a][0:64, :]),
                        reads=[bOb[a]], accw=bMd)

                NG = len(groups)
                load_qk(0)
                emit_qk(0)
                emit_qk(1)
                pending_norm = None
                for i in range(NG):
                    head, r, g = groups[i]
                    if i + 2 < NG:
                        emit_qk(i + 2)
                    if r == 1 and g == 0 and head + 1 < 8:
                        load_qk(head + 1)
                    emit_pv(i)
                    if pending_norm is not None and r == 1 and g == 2:
                        emit_norm(pending_norm)
                        pending_norm = None
                    if r == 16 and g == 7:
                        pending_norm = head
                emit_norm(pending_norm)

        def outproj_phase(l):
            with arena_scope() as ph:
                Wo = sb("Wo", [128, KC, D], BF16, ph)
                mx = [sb("mx%d" % j, [128, KC, T], BF16, ph) for j in range(2)]
                NHR = 8
                hres = [sb("hres%d" % j, [128, 2, T], F32, ph) for j in range(NHR)]
                bWo = NB()
                bMx = [NB() for _ in range(2)]
                bHres = [NB() for _ in range(NHR)]
                load_weight(Wo, wout_d[l], KC, D, "wo", bWo)
                hrp = hS.rearrange("(k p) s -> p k s", p=128)
                gbase = (l * 3 + 1) * KC

                def load_mx(t):
                    sl = t % 2
                    Sx.dma("sp", "mx%d" % sl, lambda e, t=t, sl=sl: e.dma_start(
                        out=mx[sl][:], in_=mS.rearrange("(k p) s -> p k s", p=128)[:, :, t * T:(t + 1) * T]),
                        reads=[bMd[t]], writes=[bMx[sl]])

                pairs = [(t, mp) for t in range(NT) for mp in range(KC // 2)]

                def load_pair(pi):
                    t, mp = pairs[pi]
                    hs = pi % NHR
                    Sx.dma("sp", "hro%d" % hs, lambda e, mp=mp, t=t, hs=hs: e.dma_start(
                        out=hres[hs][:], in_=hrp[:, 2 * mp:2 * mp + 2, t * T:(t + 1) * T]), reads=[bH[t]], writes=[bHres[hs]])

                load_mx(0)
                PF = 5
                for pi in range(PF):
                    load_pair(pi)
                ci = 0
                for pi, (t, mp) in enumerate(pairs):
                    sl = t % 2
                    if mp == 0 and t + 1 < NT:
                        load_mx(t + 1)
                    if pi + PF < len(pairs):
                        load_pair(pi + PF)
                    hs = pi % NHR
                    for sub in range(2):
                        m = 2 * mp + sub
                        pd = ci % 4
                        ci += 1
                        for kc in range(KC):
                            Sx.op("pe", lambda e, kc=kc, m=m, pd=pd, sl=sl: e.matmul(
                                bank(pd), lhsT=Wo[:, kc, m * 128:(m + 1) * 128], rhs=mx[sl][:, kc, :],
                                start=(kc == 0), stop=(kc == KC - 1)),
                                reads=[bWo, bMx[sl]], writes=[psB[pd]] if kc == 0 else (),
                                accw=[psB[pd]] if kc else (), sig=(kc == KC - 1))
                        Sx.op("dve", lambda e, m=m, pd=pd, hs=hs, sub=sub: e.scalar_tensor_tensor(
                            out=hres[hs][:, sub, :], in0=bank(pd), scalar=Gv[:, gbase + m:gbase + m + 1],
                            in1=hres[hs][:, sub, :], op0=ALU.mult, op1=ALU.add),
                            reads=[psB[pd], bMod, bHres[hs]], accw=[bHres[hs]])
                    Sx.dma("act", "hro%d" % hs, lambda e, mp=mp, t=t, hs=hs: e.dma_start(
                        out=hrp[:, 2 * mp:2 * mp + 2, t * T:(t + 1) * T], in_=hres[hs][:]),
                        reads=[bHres[hs]], accw=[bH[t]])

        def final_phase():
            with arena_scope() as ph:
                hb = [sb("hb%d" % j, [128, KC, T], F32, ph) for j in range(2)]
                sq = sb("sq", [128, KC, T], BF16, ph)
                lnt = sb("lnt", [128, T], F32, ph)
                rstd = sb("rstd", [128, T], F32, ph)
                ot = [sb("ot%d" % j, [128, KC, T], F32, ph) for j in range(2)]
                bHb = [NB() for _ in range(2)]
                bSq, bLn, bRs = NB(), NB(), NB()
                bOt = [NB() for _ in range(2)]

                def load_h(t):
                    Sx.dma("sp", "hb%d" % (t % 2), lambda e, t=t: e.dma_start(out=hb[t % 2][:], in_=hsrc_tile(hS, t)),
                           reads=[bH[t]], writes=[bHb[t % 2]])

                load_h(0)
                for t in range(NT):
                    sl = t % 2
                    if t + 1 < NT:
                        load_h(t + 1)
                    norm_chain_a(hb[sl], bHb[sl], sq, bSq, 0)
                    norm_chain_rstd(0, lnt, bLn, rstd, bRs)
                    for kc in range(KC):
                        Sx.op("dve", lambda e, kc=kc, sl=sl: e.scalar_tensor_tensor(
                            out=ot[sl][:, kc, :], in0=hb[sl][:, kc, :], scalar=fg_sb[:, kc:kc + 1], in1=rstd[:],
                            op0=ALU.mult, op1=ALU.mult), reads=[bHb[sl], bRs, bConst],
                            writes=[bOt[sl]] if kc == 0 else (), accw=[bOt[sl]] if kc else ())
                    out_toks.append(Sx.dma("sp", "ot%d" % sl, lambda e, t=t, sl=sl: e.dma_start(
                        out=outT.rearrange("(k p) s -> p k s", p=128)[:, :, t * T:(t + 1) * T], in_=ot[sl][:]),
                        reads=[bOt[sl]]))

        phases = []
        for l in range(DEPTH):
            phases.append(("ffn1_%d" % l, lambda l=l: ffn_phase(l, 0, xT if l == 0 else hS)))
            phases.append(("proj_%d" % l, lambda l=l: proj_phase(l)))
            phases.append(("attn_%d" % l, lambda l=l: attn_phase(l)))
            phases.append(("ffn2_%d" % l, lambda l=l: ffn_phase(l, 1, hS, pre=lambda: outproj_phase(l))))
        phases.append(("final", final_phase))
        if stop_after == "ada":
            phases = []
        for name, fn in phases:
            fn()
            if stop_after == name:
                break

        final = list(out_toks)
        for b in bH + bMd + [bQd, bKd, bVd]:
            for k, n in b.w.items():
                final.append((k[0], k[1], n))
        Sx.wait_all("sp", final)
        ok_, stuck_, val_ = Sx.check_deadlock()
        if not ok_:
            raise RuntimeError("semaphore deadlock in generated program: %r" % (stuck_,))
        Sx.run(block)
    return nc


def _perm_half(w):
    d, n = w.shape
    w4 = w.reshape(d, n // 64, 2, 32)
    return np.ascontiguousarray(w4[:, :, ::-1, :]).reshape(d, n)


def _consts():
    pos = np.arange(S, dtype=np.float32)
    inv = (np.float32(10000.0) ** (-np.arange(0, 64, 2, dtype=np.float32) / np.float32(64))).astype(np.float32)
    ang = (pos[:, None] * inv[None, :]).astype(np.float32)
    ang = np.concatenate([ang, ang], axis=-1)
    cos = np.cos(ang).astype(np.float32).T
    sin = np.sin(ang).astype(np.float32).T
    ssin = sin.copy()
    ssin[:32] *= -1.0
    rope = np.stack([np.concatenate([cos, cos], 0), np.concatenate([ssin, ssin], 0)], 0).astype(np.float32)
    k = np.arange(128)[:, None]
    i = np.arange(128)[None, :]
    tril = (k <= i).astype(np.float32)
    prev = np.where(k >= i, 0.0, NEG).astype(np.float32)
    cur = np.where(k <= i, 0.0, NEG).astype(np.float32)
    dead = np.full((128, 128), NEG, np.float32)
    mb = np.stack([np.concatenate([dead, prev, prev, prev], 1),
                   np.concatenate([prev, prev, prev, prev], 1),
                   np.concatenate([dead, prev, dead, prev], 1),
                   np.concatenate([cur, cur, cur, cur], 1)], 0).astype(np.float32)
    ident = np.eye(128, dtype=np.float32)
    sel = np.zeros((128, 64), np.float32)
    sel[64, :] = 1.0
    m_ = np.arange(128)
    partner = (m_ // 64) * 64 + np.where(m_ % 64 < 32, m_ % 64 + 32, m_ % 64 - 32)
    permh = np.zeros((128, 128), np.float32)
    permh[partner, m_] = 1.0
    return rope, tril, mb, ident, sel, permh


def _prep_inputs(x, c, ada_w, ada_b, norm_g, ffn1_wg, ffn1_wu, ffn1_wd, ffn2_wg, ffn2_wu, ffn2_wd,
                 w_in, sgu_ln_g, sgu_ln_b, sgu_w, sgu_b, w_out, final_g):
    f = lambda a: np.ascontiguousarray(np.asarray(a, dtype=np.float32))
    x, c, ada_w, ada_b, norm_g = f(x), f(c), f(ada_w), f(ada_b), f(norm_g)
    w_in = f(w_in)
    rope, tril, mb, ident, sel, permh = _consts()
    shared = {
        "ada_w": ada_w,
        "ada_b": np.ascontiguousarray(ada_b.reshape(DEPTH, 72, 128).transpose(2, 0, 1).reshape(128, DEPTH * 72)),
        "ngv": np.ascontiguousarray(norm_g.reshape(DEPTH, 3, KC, 128).transpose(3, 0, 1, 2).reshape(128, DEPTH * 3 * KC)),
        "fgv": np.ascontiguousarray(f(final_g).reshape(KC, 128).T),
        "ffn1_wg": f(ffn1_wg), "ffn1_wu": f(ffn1_wu), "ffn1_wd": f(ffn1_wd),
        "ffn2_wg": f(ffn2_wg), "ffn2_wu": f(ffn2_wu), "ffn2_wd": f(ffn2_wd),
        "w_in": w_in,
        "wsT": np.ascontiguousarray(f(sgu_w).transpose(0, 3, 1, 2)),
        "lng_b": np.ascontiguousarray(np.broadcast_to(f(sgu_ln_g).reshape(DEPTH, 1, 512), (DEPTH, 128, 512))),
        "lnb_b": np.ascontiguousarray(np.broadcast_to(f(sgu_ln_b).reshape(DEPTH, 1, 512), (DEPTH, 128, 512))),
        "bs_row": np.ascontiguousarray(f(sgu_b).reshape(DEPTH, 1, 512)),
        "w_out": f(w_out),
        "rope": rope, "tril": tril, "mbias": mb, "ident": ident, "sel65": sel, "permh": permh,
    }
    in_maps = []
    for b in range(x.shape[0]):
        m = dict(shared)
        m["xT"] = np.ascontiguousarray(x[b].T)
        m["cv"] = np.ascontiguousarray(c[b].reshape(KC, 128).T)
        in_maps.append(m)
    return in_maps


def kernel(x, c, ada_w, ada_b, norm_g, ffn1_wg, ffn1_wu, ffn1_wd, ffn2_wg, ffn2_wu, ffn2_wd,
           w_in, sgu_ln_g, sgu_ln_b, sgu_w, sgu_b, w_out, final_g):
    in_maps = _prep_inputs(x, c, ada_w, ada_b, norm_g, ffn1_wg, ffn1_wu, ffn1_wd, ffn2_wg, ffn2_wu, ffn2_wd,
                           w_in, sgu_ln_g, sgu_ln_b, sgu_w, sgu_b, w_out, final_g)
    nc = build_nc()
    res = run_bass_kernel_spmd(nc, in_maps, core_ids=list(range(8)))
    out = np.stack([np.ascontiguousarray(r["outT"].T) for r in res.results], axis=0)
    return out.astype(np.float32)
```
